# Optimizing a Trainium2 kernel written in Bass

```python
import math
import jax, jax.numpy as jnp
from jax import lax
import numpy as np


D_MODEL = 1024
BATCH = 8
SEQ = 2048
DEPTH = 2

D_FF = 2816
BRANCH_WIDTH = D_MODEL // 2
CHUNK = 128
GMLP_GROUPS = 4
GMLP_GROUP_DIM = BRANCH_WIDTH // GMLP_GROUPS
DIFF_QK_DIM = 64
DIFF_V_DIM = 2 * DIFF_QK_DIM
DIFF_HEADS = BRANCH_WIDTH // DIFF_V_DIM
Q_BLOCK = 128
S5_GROUP_DIM = 16
S5_GROUPS = BRANCH_WIDTH // S5_GROUP_DIM
S5_STATE = 64
N_BRANCH = 3
LN_EPS = 1e-5
DN_ALPHA = (2 * DEPTH) ** 0.25
DN_BETA = (8 * DEPTH) ** -0.25
QK_COLS = 4 * DIFF_HEADS * DIFF_QK_DIM
COL_SIZES = (2 * BRANCH_WIDTH, QK_COLS, BRANCH_WIDTH, BRANCH_WIDTH, N_BRANCH * D_MODEL)
IN_WIDTH = sum(COL_SIZES)
SPLIT_IDX = tuple(int(v) for v in np.cumsum(COL_SIZES)[:-1])

kernel_name = 'hybrid_gmlp_diffattn_s5_deepnorm'


def layer_norm(x, g, b):
    xf = x.astype(jnp.float32)
    mu = jnp.mean(xf, axis=-1, keepdims=True)
    var = jnp.mean(jnp.square(xf - mu), axis=-1, keepdims=True)
    return ((xf - mu) * lax.rsqrt(var + LN_EPS) * g + b).astype(x.dtype)


def swiglu(x, wg, wu, wd):
    return (jax.nn.silu(x @ wg) * (x @ wu)) @ wd


def gmlp_mixer(z, ln_g, ln_b, ws, bs):
    z = jax.nn.gelu(z)
    u, v = jnp.split(z, 2, axis=-1)
    v = layer_norm(v, ln_g, ln_b)
    bsz, l, _ = v.shape
    v = v.reshape(bsz, l // CHUNK, CHUNK, GMLP_GROUPS, GMLP_GROUP_DIM)
    mask = jnp.tril(jnp.ones((CHUNK, CHUNK), dtype=bool))
    ws = jnp.where(mask, ws, 0)
    s = jnp.einsum('gts,bcsgd->bctgd', ws, v) + bs.T[None, None, :, :, None]
    return u * s.reshape(bsz, l, BRANCH_WIDTH)


def diff_attention(q, k, v, lam, subln_g, lam_init):
    bsz, l = q.shape[0], q.shape[1]
    scale = DIFF_QK_DIM ** -0.5
    causal = jnp.tril(jnp.ones((Q_BLOCK, Q_BLOCK), dtype=bool))
    outs = []
    for i in range(l // Q_BLOCK):
        q0 = i * Q_BLOCK
        kend = q0 + Q_BLOCK
        s = jnp.einsum('bqmhd,bkmhd->bmhqk', q[:, q0:kend], k[:, :kend]).astype(jnp.float32) * scale
        mask = jnp.concatenate([jnp.ones((Q_BLOCK, q0), dtype=bool), causal], axis=1)
        p = jax.nn.softmax(jnp.where(mask, s, -jnp.inf), axis=-1)
        a = p[:, 0] - lam * p[:, 1]
        outs.append(jnp.einsum('bhqk,bkhd->bqhd', a.astype(v.dtype), v[:, :kend]))
    o = jnp.concatenate(outs, axis=1).astype(jnp.float32)
    o = o * lax.rsqrt(jnp.mean(jnp.square(o), axis=-1, keepdims=True) + LN_EPS) * subln_g * (1.0 - lam_init)
    return o.reshape(bsz, l, BRANCH_WIDTH).astype(v.dtype)


def s5_mixer(xc, lam_re, lam_im, log_step, b_re, b_im, c_re, c_im, d, glu_w, glu_b):
    f32 = jnp.float32
    bsz, l, _ = xc.shape
    u = xc.astype(f32).reshape(bsz, l, S5_GROUPS, S5_GROUP_DIM)
    lr = lam_re.astype(f32)
    li = lam_im.astype(f32)
    step = jnp.exp(log_step.astype(f32))[:, None]
    mag = jnp.exp(lr * step)
    ar = mag * jnp.cos(li * step)
    ai = mag * jnp.sin(li * step)
    den = lr * lr + li * li
    cr = ((ar - 1.0) * lr + ai * li) / den
    ci = (ai * lr - (ar - 1.0) * li) / den
    br = b_re.astype(f32)
    bi = b_im.astype(f32)
    bbar_re = cr[..., None] * br - ci[..., None] * bi
    bbar_im = cr[..., None] * bi + ci[..., None] * br
    bu_re = jnp.einsum('blgh,gph->blgp', u, bbar_re)
    bu_im = jnp.einsum('blgh,gph->blgp', u, bbar_im)
    a_re = jnp.broadcast_to(ar, bu_re.shape)
    a_im = jnp.broadcast_to(ai, bu_im.shape)

    def combine(e1, e2):
        a1r, a1i, b1r, b1i = e1
        a2r, a2i, b2r, b2i = e2
        return (a2r * a1r - a2i * a1i,
                a2r * a1i + a2i * a1r,
                a2r * b1r - a2i * b1i + b2r,
                a2r * b1i + a2i * b1r + b2i)

    _, _, hr, hi = lax.associative_scan(combine, (a_re, a_im, bu_re, bu_im), axis=1)
    y = (jnp.einsum('blgp,ghp->blgh', hr, c_re.astype(f32))
         - jnp.einsum('blgp,ghp->blgh', hi, c_im.astype(f32))
         + d.astype(f32) * u)
    y = jax.nn.gelu(y.reshape(bsz, l, BRANCH_WIDTH)).astype(xc.dtype)
    return y * jax.nn.sigmoid(y @ glu_w + glu_b)


def setup_inputs(seed: int = 0) -> dict:
    key = jax.random.key(seed)
    ks = iter(jax.random.split(key, 48))

    def nrm(shape, scale):
        return jax.random.normal(next(ks), shape, jnp.float32) * scale

    def gain(shape):
        return 1.0 + nrm(shape, 0.02)

    L = DEPTH
    W = BRANCH_WIDTH
    G, P, H = S5_GROUPS, S5_STATE, S5_GROUP_DIM
    inp = {}
    inp['x'] = nrm((BATCH, SEQ, D_MODEL), 1.0)
    inp['ffn1_gate'] = nrm((L, D_MODEL, D_FF), D_MODEL ** -0.5)
    inp['ffn1_up'] = nrm((L, D_MODEL, D_FF), D_MODEL ** -0.5)
    inp['ffn1_down'] = nrm((L, D_FF, D_MODEL), D_FF ** -0.5 * DN_BETA)
    inp['ln1_g'] = gain((L, D_MODEL))
    inp['ln1_b'] = nrm((L, D_MODEL), 0.02)
    inp['w_in'] = nrm((L, D_MODEL, IN_WIDTH), D_MODEL ** -0.5)
    inp['gmlp_ln_g'] = gain((L, W))
    inp['gmlp_ln_b'] = nrm((L, W), 0.02)
    inp['gmlp_ws'] = nrm((L, GMLP_GROUPS, CHUNK, CHUNK), 0.5 * CHUNK ** -0.5)
    inp['gmlp_bs'] = 1.0 + nrm((L, GMLP_GROUPS, CHUNK), 0.1)
    inp['diff_lq1'] = nrm((L, DIFF_QK_DIM), 0.1)
    inp['diff_lk1'] = nrm((L, DIFF_QK_DIM), 0.1)
    inp['diff_lq2'] = nrm((L, DIFF_QK_DIM), 0.1)
    inp['diff_lk2'] = nrm((L, DIFF_QK_DIM), 0.1)
    inp['diff_subln_g'] = gain((L, DIFF_V_DIM))
    inp['s5_lambda_re'] = -0.5 + nrm((L, G, P), 0.01)
    inp['s5_lambda_im'] = math.pi * jnp.arange(P, dtype=jnp.float32) + nrm((L, G, P), 0.01)
    inp['s5_log_step'] = jax.random.uniform(next(ks), (L, G), jnp.float32, math.log(1e-3), math.log(1e-1))
    inp['s5_b_re'] = nrm((L, G, P, H), (2 * H) ** -0.5)
    inp['s5_b_im'] = nrm((L, G, P, H), (2 * H) ** -0.5)
    inp['s5_c_re'] = nrm((L, G, H, P), P ** -0.5)
    inp['s5_c_im'] = nrm((L, G, H, P), P ** -0.5)
    inp['s5_d'] = nrm((L, G, H), 0.5)
    inp['s5_glu_w'] = nrm((L, W, W), W ** -0.5)
    inp['s5_glu_b'] = nrm((L, W), 0.02)
    inp['w_branch'] = nrm((L, N_BRANCH, W, D_MODEL), W ** -0.5)
    inp['w_out'] = nrm((L, D_MODEL, D_MODEL), D_MODEL ** -0.5 * DN_BETA)
    inp['ln2_g'] = gain((L, D_MODEL))
    inp['ln2_b'] = nrm((L, D_MODEL), 0.02)
    inp['ffn2_gate'] = nrm((L, D_MODEL, D_FF), D_MODEL ** -0.5)
    inp['ffn2_up'] = nrm((L, D_MODEL, D_FF), D_MODEL ** -0.5)
    inp['ffn2_down'] = nrm((L, D_FF, D_MODEL), D_FF ** -0.5 * DN_BETA)
    inp['ln3_g'] = gain((L, D_MODEL))
    inp['ln3_b'] = nrm((L, D_MODEL), 0.02)
    return inp


def reference(x, ffn1_gate, ffn1_up, ffn1_down, ln1_g, ln1_b, w_in, gmlp_ln_g, gmlp_ln_b,
              gmlp_ws, gmlp_bs, diff_lq1, diff_lk1, diff_lq2, diff_lk2, diff_subln_g,
              s5_lambda_re, s5_lambda_im, s5_log_step, s5_b_re, s5_b_im, s5_c_re, s5_c_im,
              s5_d, s5_glu_w, s5_glu_b, w_branch, w_out, ln2_g, ln2_b,
              ffn2_gate, ffn2_up, ffn2_down, ln3_g, ln3_b):
    bsz, l, _ = x.shape
    for i in range(DEPTH):
        x = layer_norm(DN_ALPHA * x + 0.5 * swiglu(x, ffn1_gate[i], ffn1_up[i], ffn1_down[i]), ln1_g[i], ln1_b[i])

        h = x @ w_in[i]
        z_gmlp, z_qk, z_v, z_s5, z_gate = jnp.split(h, SPLIT_IDX, axis=-1)

        y_a = gmlp_mixer(z_gmlp, gmlp_ln_g[i], gmlp_ln_b[i], gmlp_ws[i], gmlp_bs[i])

        qk = z_qk.reshape(bsz, l, 4, DIFF_HEADS, DIFF_QK_DIM)
        v = z_v.reshape(bsz, l, DIFF_HEADS, DIFF_V_DIM)
        lam_init = 0.8 - 0.6 * math.exp(-0.3 * i)
        f32 = jnp.float32
        lam = (jnp.exp(jnp.sum(diff_lq1[i].astype(f32) * diff_lk1[i].astype(f32)))
               - jnp.exp(jnp.sum(diff_lq2[i].astype(f32) * diff_lk2[i].astype(f32))) + lam_init)
        y_b = diff_attention(qk[:, :, 0:2], qk[:, :, 2:4], v, lam, diff_subln_g[i], lam_init)

        y_c = s5_mixer(z_s5, s5_lambda_re[i], s5_lambda_im[i], s5_log_step[i], s5_b_re[i], s5_b_im[i],
                       s5_c_re[i], s5_c_im[i], s5_d[i], s5_glu_w[i], s5_glu_b[i])

        branches = jnp.stack([y_a, y_b, y_c], axis=2)
        proj = jnp.einsum('blkw,kwd->blkd', branches, w_branch[i])
        gates = jax.nn.sigmoid(z_gate.reshape(bsz, l, N_BRANCH, D_MODEL))
        mix = jnp.sum(gates * proj, axis=2) @ w_out[i]
        x = layer_norm(DN_ALPHA * x + mix, ln2_g[i], ln2_b[i])

        x = layer_norm(DN_ALPHA * x + 0.5 * swiglu(x, ffn2_gate[i], ffn2_up[i], ffn2_down[i]), ln3_g[i], ln3_b[i])
    return x
```

```python
import math
from contextlib import ExitStack
import numpy as np
import concourse.bass as bass
import concourse.mybir as mybir
from concourse.bass_utils import run_bass_kernel_spmd

F32 = mybir.dt.float32
BF16 = mybir.dt.bfloat16
I32 = mybir.dt.int32
AF = mybir.ActivationFunctionType
ALU = mybir.AluOpType

D = 1024
T = 2048
FF = 2816
NJ = 22
DEPTH = 2
LN_EPS = 1e-5
ALPHA = (2 * DEPTH) ** 0.25
SLOT = 2048
NS = 4
HOLD = 2
TC = 128

R_AWV, R_GWV, R_B, R_C, R_D, R_GLU = 0, 4096, 8192, 12288, 16384, 16896
RESW = 18944
SP_LN = 0
SP_GLNG = 48
SP_GLNB = 560
SP_GBS = 1072
SP_LQK = 1584
SP_SUB = 1840
SP_S5 = 1842
SP_GLUB = 1890
SP_WST = 1894
SP = 2408
NCONST = 6 * 128


def _kmaj(w):
    k = w.shape[0] // 128
    n = w.shape[1]
    return w.reshape(k, 128, n).transpose(1, 0, 2).reshape(128, k * n)


def layer_chunk_lens():
    lens = []
    for _ffn in range(1):
        pass

    def ffn():
        l = []
        for half in range(2):
            l += [2048] * NJ
            for m in range(8):
                l += [1408, 1408]
        return l

    lens += ffn()
    lens += [2048] * 2
    lens += [2048] * 4
    lens += [2048] * 2
    for tp in range(2):
        lens += [1536] * 24
        lens += [2048] * 8
    lens += ffn()
    return lens


def pack_layer_stream(W, i):
    ch = []

    def ffn(pref):
        wg, wu, wd = W[pref + "_gate"][i], W[pref + "_up"][i], W[pref + "_down"][i]
        ups = [np.concatenate([_kmaj(wg[:, j * 128:(j + 1) * 128]), _kmaj(wu[:, j * 128:(j + 1) * 128])], axis=1)
               for j in range(NJ)]
        downs = []
        for m in range(8):
            for jh in range(2):
                blk = wd[jh * 1408:(jh + 1) * 1408, m * 128:(m + 1) * 128]
                downs.append(_kmaj(blk))
        for half in range(2):
            ch.extend(ups)
            ch.extend(downs)

    w_in = W["w_in"][i]
    ffn("ffn1")
    for c in range(2):
        ch.append(_kmaj(w_in[:, 2560 + 256 * c:2560 + 256 * (c + 1)]))
    for h in range(4):
        cols = [w_in[:, 1024 + wh * 256 + h * 64:1024 + wh * 256 + (h + 1) * 64] for wh in range(4)]
        ch.append(_kmaj(np.concatenate(cols, axis=1)))
    for c in range(2):
        ch.append(_kmaj(w_in[:, 256 * c:256 * (c + 1)]))
    fin = []
    for m in range(8):
        for k in range(3):
            wb = _kmaj(W["w_branch"][i, k][:, m * 128:(m + 1) * 128])
            wgt = _kmaj(w_in[:, 3072 + k * 1024 + m * 128:3072 + k * 1024 + (m + 1) * 128])
            fin.append(np.concatenate([wb, wgt], axis=1))
    wo = [_kmaj(W["w_out"][i][:, 256 * c:256 * (c + 1)]) for c in range(4)]
    for tp in range(2):
        ch.extend(fin)
        ch.extend(wo)
        ch.extend(wo)
    ffn("ffn2")
    return ch


def pack_resident(W, i):
    r = np.zeros((128, RESW), np.float32)
    w_in = W["w_in"][i]
    r[:, R_AWV:R_AWV + 4096] = _kmaj(w_in[:, 2048:2560])
    r[:, R_GWV:R_GWV + 4096] = _kmaj(w_in[:, 512:1024])
    Bt = np.zeros((128, 16, 2, 128), np.float32)
    Ct = np.zeros((128, 4, 4, 2, 128), np.float32)
    bre, bim = W["s5_b_re"][i], W["s5_b_im"][i]
    cre, cim = W["s5_c_re"][i], W["s5_c_im"][i]
    for j in range(16):
        o, jj = j // 4, j % 4
        for g2 in range(2):
            g = 2 * j + g2
            r0 = (2 * jj + g2) * 16
            Bt[r0:r0 + 16, j, 0, g2 * 64:(g2 + 1) * 64] = bre[g].T
            Bt[r0:r0 + 16, j, 1, g2 * 64:(g2 + 1) * 64] = bim[g].T
            Ct[g2 * 64:(g2 + 1) * 64, o, jj, 0, r0:r0 + 16] = cre[g].T
            Ct[g2 * 64:(g2 + 1) * 64, o, jj, 1, r0:r0 + 16] = cim[g].T
    r[:, R_B:R_B + 4096] = Bt.reshape(128, 4096)
    r[:, R_C:R_C + 4096] = Ct.reshape(128, 4096)
    Dt = np.zeros((128, 4, 128), np.float32)
    dflat = W["s5_d"][i].reshape(512)
    for o in range(4):
        Dt[np.arange(128), o, np.arange(128)] = dflat[o * 128:(o + 1) * 128]
    r[:, R_D:R_D + 512] = Dt.reshape(128, 512)
    r[:, R_GLU:R_GLU + 2048] = _kmaj(W["s5_glu_w"][i])
    return r


def pack_small(W, i):
    s = np.zeros((128, SP), np.float32)
    for n, nm in enumerate(["ln1_g", "ln1_b", "ln2_g", "ln2_b", "ln3_g", "ln3_b"]):
        s[:, SP_LN + 8 * n:SP_LN + 8 * (n + 1)] = W[nm][i].reshape(8, 128).T
    s[:, SP_GLNG:SP_GLNG + 512] = np.broadcast_to(W["gmlp_ln_g"][i][None, :], (128, 512))
    s[:, SP_GLNB:SP_GLNB + 512] = np.broadcast_to(W["gmlp_ln_b"][i][None, :], (128, 512))
    s[:, SP_GBS:SP_GBS + 512] = np.broadcast_to(W["gmlp_bs"][i].reshape(1, 512), (128, 512))
    for n, nm in enumerate(["diff_lq1", "diff_lk1", "diff_lq2", "diff_lk2"]):
        s[:, SP_LQK + 64 * n:SP_LQK + 64 * (n + 1)] = np.broadcast_to(W[nm][i][None, :], (128, 64))
    s[:, SP_SUB] = W["diff_subln_g"][i]
    s[:, SP_S5:SP_S5 + 16] = W["s5_lambda_re"][i].reshape(16, 128).T
    s[:, SP_S5 + 16:SP_S5 + 32] = W["s5_lambda_im"][i].reshape(16, 128).T
    s[:, SP_S5 + 32:SP_S5 + 48] = np.repeat(W["s5_log_step"][i], 64).reshape(16, 128).T
    s[:, SP_GLUB:SP_GLUB + 4] = W["s5_glu_b"][i].reshape(4, 128).T
    s[:, SP_WST:SP_WST + 512] = W["gmlp_ws"][i].transpose(2, 0, 1).reshape(128, 512)
    return s


def make_consts():
    c = np.zeros((128, NCONST), np.float32)
    c[:, 0:128] = 1.0 / 1024.0
    c[:, 128:256] = 1.0 / 128.0
    c[:, 256:384] = 1.0
    c[:, 384:512] = np.eye(128, dtype=np.float32)
    k = np.arange(128)[:, None]
    q = np.arange(128)[None, :]
    c[:, 512:640] = (k <= q).astype(np.float32)
    c[:, 640:768] = np.where(k > q, -30000.0, 0.0).astype(np.float32)
    return c


class Res:
    __slots__ = ("w", "r")

    def __init__(self):
        self.w = None
        self.r = {}


class KB:
    def __init__(self, nc):
        self.nc = nc
        self.E = dict(pe=nc.tensor, act=nc.scalar, dve=nc.vector, pool=nc.gpsimd, sp=nc.sync)
        self.sem = {}
        self.cnt = {}
        self.waited = {e: {} for e in self.E}
        for e in self.E:
            self.newsem(e)

    def newsem(self, key):
        self.sem[key] = self.nc.alloc_semaphore("s_" + key)
        self.cnt[key] = 0

    def _deps(self, eng, reads, writes):
        deps = {}
        for r in reads:
            if r.w is not None:
                k, v = r.w
                if v > deps.get(k, 0):
                    deps[k] = v
        for w in writes:
            if w.w is not None:
                k, v = w.w
                if v > deps.get(k, 0):
                    deps[k] = v
            for k, v in w.r.items():
                if v > deps.get(k, 0):
                    deps[k] = v
        if eng == "pe":
            deps.pop("pe", None)
        return deps

    def _wait(self, eng, deps):
        wd = self.waited[eng]
        for k, v in deps.items():
            if wd.get(k, 0) >= v:
                continue
            self.E[eng].wait_ge(self.sem[k], v)
            wd[k] = v

    def _mark(self, tok, reads, writes):
        k, v = tok
        for w in writes:
            w.w = tok
            w.r = {}
        for r in reads:
            if r.r.get(k, 0) < v:
                r.r[k] = v

    def op(self, eng, fn, reads=(), writes=()):
        self._wait(eng, self._deps(eng, reads, writes))
        inst = fn(self.E[eng])
        self.cnt[eng] += 1
        inst.then_inc(self.sem[eng], 1)
        self._mark((eng, self.cnt[eng]), reads, writes)

    def mm(self, groups, reads=(), writes=()):
        self._wait("pe", self._deps("pe", reads, writes))
        inst = None
        for out_ap, pairs in groups:
            n = len(pairs)
            for i, (l, r) in enumerate(pairs):
                inst = self.nc.tensor.matmul(out_ap, lhsT=l, rhs=r, start=(i == 0), stop=(i == n - 1))
        self.cnt["pe"] += 1
        inst.then_inc(self.sem["pe"], 1)
        self._mark(("pe", self.cnt["pe"]), reads, writes)

    def dma(self, q, out, in_, semkey, reads=(), writes=()):
        if semkey not in self.sem:
            self.newsem(semkey)
        self._wait(q, self._deps(q, reads, writes))
        inst = self.E[q].dma_start(out=out, in_=in_)
        self.cnt[semkey] += 16
        inst.then_inc(self.sem[semkey], 16)
        self._mark((semkey, self.cnt[semkey]), reads, writes)

    def barrier(self):
        for e in self.E:
            deps = {k: v for k, v in self.cnt.items() if k != e and v > 0}
            self._wait(e, deps)

    def wait_all(self, eng):
        deps = {k: v for k, v in self.cnt.items() if k != eng and v > 0}
        self._wait(eng, deps)


class WStream:
    def __init__(self, kb, wflat, lens):
        self.kb = kb
        self.wflat = wflat
        self.lens = lens
        self.offs = np.concatenate([[0], np.cumsum(lens)]).astype(np.int64)
        self.slots = [kb.nc.alloc_sbuf_tensor("wring%d" % s, [128, SLOT], BF16) for s in range(NS)]
        self.res = [Res() for _ in range(NS)]
        self.next_load = 0
        self.next_get = 0

    def get(self, expect_len=None):
        q = self.next_get
        self.next_get += 1
        lim = min(q + NS - HOLD + 1, len(self.lens))
        while self.next_load < lim:
            p = self.next_load
            s = p % NS
            o, l = int(self.offs[p]), int(self.lens[p])
            self.kb.dma("pool", self.slots[s][:, 0:l], self.wflat[:, o:o + l], "w%d" % s, writes=[self.res[s]])
            self.next_load += 1
        if expect_len is not None:
            assert self.lens[q] == expect_len, (q, self.lens[q], expect_len)
        return self.slots[q % NS], self.res[q % NS]


def build_program(nlayers=DEPTH, stop=None):
    nc = bass.Bass("TRN2", target_bir_lowering=False)
    kb = KB(nc)
    lens1 = layer_chunk_lens()
    lens = lens1 * nlayers
    TOT = int(sum(lens))
    xT = nc.dram_tensor("xT", [D, T], F32, kind="ExternalInput").ap()
    wflat = nc.dram_tensor("wflat", [128, TOT], F32, kind="ExternalInput").ap()
    resw = nc.dram_tensor("resw", [nlayers, 128, RESW], F32, kind="ExternalInput").ap()
    smallp = nc.dram_tensor("smallp", [nlayers, 128, SP], F32, kind="ExternalInput").ap()
    constd = nc.dram_tensor("consts", [128, NCONST], F32, kind="ExternalInput").ap()
    outT = nc.dram_tensor("outT", [D, T], F32, kind="ExternalOutput").ap()

    xb = nc.alloc_sbuf_tensor("xb", [128, 8, T], BF16)
    xlo = nc.alloc_sbuf_tensor("xlo", [128, 8, T], BF16)
    xres = [[Res() for _ in range(4)] for _ in range(8)]
    ws = WStream(kb, wflat, lens)
    cb = nc.alloc_sbuf_tensor("cb", [128, NCONST], BF16)
    cres = Res()
    SPB = SP - 1536
    sp_ = nc.alloc_sbuf_tensor("sp", [128, SPB], F32)
    spres = Res()

    class _SPV:
        def __getitem__(self, key):
            p, c = key
            a, b = c.start, c.stop
            assert a < SP_GLNG or a >= SP_LQK, (a, b)
            if a >= SP_LQK:
                a, b = a - 1536, b - 1536
            return sp_[p, a:b]
    sp = _SPV()
    PS = [nc.alloc_psum_tensor("ps%d" % i, [128, 512], F32) for i in range(8)]
    PR = [Res() for _ in range(8)]
    onesD = cb[:, 0:128]
    ones128 = cb[:, 128:256]
    ones1 = cb[:, 256:384]
    tri = cb[:, 512:640]
    ident = cb[:, 384:512]
    maskneg = cb[:, 640:768]

    def tsl(tt):
        return slice(tt * 512, (tt + 1) * 512)

    uid = [0]

    def SBT(name, shape, dt):
        uid[0] += 1
        return nc.sbuf_tensor("%s_u%d" % (name, uid[0]), shape, dt)

    kb.dma("pool", cb[:], constd, "cst", writes=[cres])

    with ExitStack() as es:
        xt = [es.enter_context(SBT("xld%d" % i, [128, 512], F32)) for i in range(4)]
        xtr = [Res() for _ in range(4)]
        n = 0
        for c in range(8):
            for tt in range(4):
                b = n % 4
                n += 1
                kb.dma("sp", xt[b][:], xT[c * 128:(c + 1) * 128, tsl(tt)], "xl%d" % b, writes=[xtr[b]])
                kb.op("act", lambda e, b=b, c=c, tt=tt: e.activation(out=xb[:, c, tsl(tt)], in_=xt[b][:], func=AF.Copy),
                      reads=[xtr[b]], writes=[xres[c][tt]])
                kb.op("dve", lambda e, b=b, c=c, tt=tt: e.tensor_tensor(out=xlo[:, c, tsl(tt)], in0=xt[b][:],
                                                                       in1=xb[:, c, tsl(tt)], op=ALU.subtract),
                      reads=[xtr[b], xres[c][tt]], writes=[xres[c][tt]])
        kb.barrier()

    def resid_evac(rt, rres, m, tt, ps, psr, coef):
        kb.op("dve", lambda e: e.scalar_tensor_tensor(out=rt[:, m, :], in0=ps[:], scalar=coef, in1=xb[:, m, tsl(tt)],
                                                      op0=ALU.mult, op1=ALU.add),
              reads=[psr, xres[m][tt]], writes=[rres[m]])
        kb.op("dve", lambda e: e.tensor_tensor(out=rt[:, m, :], in0=rt[:, m, :], in1=xlo[:, m, tsl(tt)], op=ALU.add),
              reads=[rres[m], xres[m][tt]], writes=[rres[m]])

    class BG:
        def __init__(self):
            self.q = []

        def add(self, steps):
            self.q.extend(steps)

        def pump(self, n=1):
            for _ in range(n):
                if self.q:
                    self.q.pop(0)()

        def drain(self):
            while self.q:
                self.q.pop(0)()

    def ln_steps(tt, rt, rres, T_, gi, final):
        eps = LN_EPS / (ALPHA * ALPHA)
        gcol = sp[:, SP_LN + 16 * gi:SP_LN + 16 * gi + 8]
        bcol = sp[:, SP_LN + 16 * gi + 8:SP_LN + 16 * gi + 16]
        rb, rsq, mu, rstd, tmp, xf = T_["rb"], T_["rsq"], T_["mu"], T_["rstd"], T_["tmp"], T_["xf"]
        sr = T_["sr"]

        def stats(c):
            s = c % 2
            kb.op("dve", lambda e: e.tensor_copy(out=rb[s][:], in_=rt[:, c, :]), reads=[rres[c]], writes=[T_["rbr"][s]])
            kb.op("act", lambda e: e.activation(out=rsq[s][:], in_=rt[:, c, :], func=AF.Square),
                  reads=[rres[c]], writes=[T_["rsqr"][s]])
            kb._wait("pe", kb._deps("pe", [T_["rbr"][s], T_["rsqr"][s], cres], [PR[6], PR[7]] if c == 0 else []))
            nc.tensor.matmul(PS[6][:], lhsT=onesD, rhs=rb[s][:], start=(c == 0), stop=(c == 7))
            i2 = nc.tensor.matmul(PS[7][:], lhsT=onesD, rhs=rsq[s][:], start=(c == 0), stop=(c == 7))
            kb.cnt["pe"] += 1
            i2.then_inc(kb.sem["pe"], 1)
            tok = ("pe", kb.cnt["pe"])
            kb._mark(tok, [T_["rbr"][s], T_["rsqr"][s], cres], [PR[6], PR[7]] if c == 7 else [])
            if c != 7:
                PR[6].w = tok
                PR[7].w = tok

        def stats_lo():
            for c in range(4):
                stats(c)

        def stats_hi():
            for c in range(4, 8):
                stats(c)

        def chain():
            kb.op("dve", lambda e: e.tensor_copy(out=mu[:], in_=PS[6][:]), reads=[PR[6]], writes=[sr])
            kb.op("dve", lambda e: e.tensor_tensor(out=rstd[:], in0=mu[:], in1=mu[:], op=ALU.mult), reads=[sr], writes=[sr])
            kb.op("dve", lambda e: e.tensor_tensor(out=rstd[:], in0=PS[7][:], in1=rstd[:], op=ALU.subtract),
                  reads=[sr, PR[7]], writes=[sr])
            kb.op("dve", lambda e: e.tensor_scalar(out=rstd[:], in0=rstd[:], scalar1=eps, scalar2=None, op0=ALU.add),
                  reads=[sr], writes=[sr])
            kb.op("act", lambda e: e.activation(out=rstd[:], in_=rstd[:], func=AF.Ln), reads=[sr], writes=[sr])
            kb.op("act", lambda e: e.activation(out=rstd[:], in_=rstd[:], func=AF.Exp, scale=-0.5), reads=[sr], writes=[sr])
            kb.op("dve", lambda e: e.scalar_tensor_tensor(out=mu[:], in0=mu[:], scalar=-1.0, in1=rstd[:], op0=ALU.mult,
                                                          op1=ALU.mult), reads=[sr], writes=[sr])

        def norm(c):
            s = c % 2
            kb.op("dve", lambda e: e.tensor_tensor(out=tmp[s][:], in0=rt[:, c, :], in1=rstd[:], op=ALU.mult),
                  reads=[rres[c], sr], writes=[T_["tmpr"][s]])
            kb.op("dve", lambda e: e.tensor_tensor(out=tmp[s][:], in0=tmp[s][:], in1=mu[:], op=ALU.add),
                  reads=[T_["tmpr"][s], sr], writes=[T_["tmpr"][s]])
            kb.op("act", lambda e: e.activation(out=xf[s][:], in_=tmp[s][:], func=AF.Identity,
                                                scale=gcol[:, c:c + 1], bias=bcol[:, c:c + 1]),
                  reads=[T_["tmpr"][s], spres], writes=[T_["xfr"][s]])
            if final:
                kb.dma("sp", outT[c * 128:(c + 1) * 128, tsl(tt)], xf[s][:], "out", reads=[T_["xfr"][s]])
            else:
                kb.op("act", lambda e: e.activation(out=xb[:, c, tsl(tt)], in_=tmp[s][:], func=AF.Identity,
                                                    scale=gcol[:, c:c + 1], bias=bcol[:, c:c + 1]),
                      reads=[T_["tmpr"][s], spres], writes=[xres[c][tt]])
                kb.op("dve", lambda e: e.tensor_tensor(out=xlo[:, c, tsl(tt)], in0=xf[s][:], in1=xb[:, c, tsl(tt)],
                                                       op=ALU.subtract),
                      reads=[T_["xfr"][s], xres[c][tt]], writes=[xres[c][tt]])
        steps = [stats_lo, stats_hi, chain]
        for c in range(8):
            steps.append(lambda c=c: norm(c))
        return steps

    def ln_tile(tt, rt, rres, T_, gi, final):
        for st in ln_steps(tt, rt, rres, T_, gi, final):
            st()

    def ln_temps(es):
        T_ = {}
        T_["rb"] = [es.enter_context(SBT("ln_rb%d" % i, [128, 512], BF16)) for i in range(2)]
        T_["rsq"] = [es.enter_context(SBT("ln_rsq%d" % i, [128, 512], BF16)) for i in range(2)]
        T_["tmp"] = [es.enter_context(SBT("ln_tmp%d" % i, [128, 512], F32)) for i in range(2)]
        T_["xf"] = [es.enter_context(SBT("ln_xf%d" % i, [128, 512], F32)) for i in range(2)]
        T_["mu"] = es.enter_context(SBT("ln_mu", [128, 512], F32))
        T_["rstd"] = es.enter_context(SBT("ln_rstd", [128, 512], F32))
        for k in ("rbr", "rsqr", "tmpr", "xfr"):
            T_[k] = [Res(), Res()]
        T_["sr"] = Res()
        return T_

    def ffn(gi, final):
        with ExitStack() as es:
            h = es.enter_context(SBT("ffn_h", [128, NJ, 1024], BF16))
            sg = [es.enter_context(SBT("ffn_sg%d" % i, [128, 512], BF16)) for i in range(2)]
            rt2 = [es.enter_context(SBT("ffn_rt%d" % i, [128, 8, 512], F32)) for i in range(2)]
            T_ = ln_temps(es)
            hres = [[Res(), Res()] for _ in range(NJ)]
            sgr = [Res(), Res()]
            rres2 = [[Res() for _ in range(8)] for _ in range(2)]
            it = 0
            bg = BG()
            for half in range(2):
                for j in range(NJ):
                    bg.pump(1)
                    wt, wr = ws.get(2048)
                    for tt2 in range(2):
                        tt = 2 * half + tt2
                        b = it % 2
                        it += 1
                        xr = [xres[k][tt] for k in range(8)]
                        kb.mm([(PS[b][:], [(wt[:, k * 128:(k + 1) * 128], xb[:, k, tsl(tt)]) for k in range(8)])],
                              reads=[wr] + xr, writes=[PR[b]])
                        kb.mm([(PS[2 + b][:], [(wt[:, 1024 + k * 128:1024 + (k + 1) * 128], xb[:, k, tsl(tt)])
                                               for k in range(8)])], reads=[wr] + xr, writes=[PR[2 + b]])
                        kb.op("act", lambda e, b=b: e.activation(out=sg[b][:], in_=PS[b][:], func=AF.Silu),
                              reads=[PR[b]], writes=[sgr[b]])
                        kb.op("dve", lambda e, b=b, j=j, tt2=tt2: e.tensor_tensor(
                            out=h[:, j, tt2 * 512:(tt2 + 1) * 512], in0=PS[2 + b][:], in1=sg[b][:], op=ALU.mult),
                            reads=[PR[2 + b], sgr[b]], writes=[hres[j][tt2]])
                for m in range(8):
                    wa, war = ws.get(1408)
                    wb_, wbr = ws.get(1408)
                    for tt2 in range(2):
                        tt = 2 * half + tt2
                        b = 4 + tt2
                        pairs = [(wa[:, jj * 128:(jj + 1) * 128], h[:, jj, tt2 * 512:(tt2 + 1) * 512]) for jj in range(11)]
                        pairs += [(wb_[:, jj * 128:(jj + 1) * 128], h[:, 11 + jj, tt2 * 512:(tt2 + 1) * 512])
                                  for jj in range(11)]
                        kb.mm([(PS[b][:], pairs)], reads=[war, wbr] + [hres[j][tt2] for j in range(NJ)], writes=[PR[b]])
                        resid_evac(rt2[tt2], rres2[tt2], m, tt, PS[b], PR[b], 0.5 / ALPHA)
                bg.drain()
                for tt2 in range(2):
                    bg.add(ln_steps(2 * half + tt2, rt2[tt2], rres2[tt2], T_, gi, final))
            bg.drain()
            kb.barrier()

    def mixer(li):
        lam_init = 0.8 - 0.6 * math.exp(-0.3 * li)
        with ExitStack() as es0:
            yc = es0.enter_context(SBT("y_c", [128, 4, T], BF16))
            ycr = [[Res() for _ in range(4)] for _ in range(4)]
            s5_branch(li, yc, ycr)
            kb.barrier()
            yb = es0.enter_context(SBT("y_b", [128, 4, T], BF16))
            ybr = [[Res() for _ in range(4)] for _ in range(4)]
            attn_branch(li, lam_init, yb, ybr)
            kb.barrier()
            ya = es0.enter_context(SBT("y_a", [128, 4, T], BF16))
            yar = [[Res() for _ in range(4)] for _ in range(4)]
            gmlp_branch(li, ya, yar)
            kb.barrier()
            final_stage(li, [ya, yb, yc], [yar, ybr, ycr])
            kb.barrier()

    def s5_branch(li, yc, ycr):
        with ExitStack() as es:
            def sb(name, shape, dt):
                return es.enter_context(SBT(name, shape, dt))
            uT = yc
            uTr = ycr
            Er = sb("s5_Er", [128, 16, TC], F32)
            Ei = sb("s5_Ei", [128, 16, TC], F32)
            Fr = sb("s5_Fr", [128, 16, TC], F32)
            Fi = sb("s5_Fi", [128, 16, TC], F32)
            Bt = sb("s5_B", [128, 16, 2, 128], BF16)
            Ct = sb("s5_C", [128, 4, 4, 2, 128], BF16)
            Dt = sb("s5_D", [128, 4, 128], BF16)
            glu = sb("s5_glu", [128, 4, 512], BF16)
            rwr = Res()
            kb.dma("pool", Bt[:].rearrange("p a b c -> p (a b c)"), resw[li, :, R_B:R_B + 4096], "rw0", writes=[rwr])
            kb.dma("pool", Ct[:].rearrange("p a b c d -> p (a b c d)"), resw[li, :, R_C:R_C + 4096], "rw0", writes=[rwr])
            kb.dma("pool", Dt[:].rearrange("p a b -> p (a b)"), resw[li, :, R_D:R_D + 512], "rw0", writes=[rwr])
            kb.dma("pool", glu[:].rearrange("p a b -> p (a b)"), resw[li, :, R_GLU:R_GLU + 2048], "rw0", writes=[rwr])
            P = {}
            for nm in ["step", "mag", "th", "kf", "ph", "phc", "m", "cs", "sn", "ar", "ai", "den", "am1", "cr", "ci",
                       "t0", "t1", "hre", "him"]:
                P[nm] = sb("s5p_" + nm, [128, 16], F32)
            ki = sb("s5p_ki", [128, 16], I32)
            pr = Res()
            lr = sp[:, SP_S5:SP_S5 + 16]
            lim = sp[:, SP_S5 + 16:SP_S5 + 32]
            lsg = sp[:, SP_S5 + 32:SP_S5 + 48]

            def dv(fn):
                kb.op("dve", fn, reads=[pr, spres], writes=[pr])

            def ac(fn):
                kb.op("act", fn, reads=[pr, spres], writes=[pr])
            TWO_PI = 2.0 * math.pi
            C1 = 6.28125
            C2 = TWO_PI - C1
            ac(lambda e: e.activation(out=P["step"][:], in_=lsg, func=AF.Exp))
            dv(lambda e: e.tensor_tensor(out=P["t0"][:], in0=lr, in1=P["step"][:], op=ALU.mult))
            ac(lambda e: e.activation(out=P["mag"][:], in_=P["t0"][:], func=AF.Exp))
            dv(lambda e: e.tensor_tensor(out=P["th"][:], in0=lim, in1=P["step"][:], op=ALU.mult))
            dv(lambda e: e.tensor_scalar(out=P["kf"][:], in0=P["th"][:], scalar1=1.0 / TWO_PI, scalar2=0.5,
                                         op0=ALU.mult, op1=ALU.add))
            dv(lambda e: e.tensor_copy(out=ki[:], in_=P["kf"][:]))
            dv(lambda e: e.tensor_copy(out=P["kf"][:], in_=ki[:]))
            dv(lambda e: e.scalar_tensor_tensor(out=P["ph"][:], in0=P["kf"][:], scalar=-C1, in1=P["th"][:],
                                                op0=ALU.mult, op1=ALU.add))
            dv(lambda e: e.scalar_tensor_tensor(out=P["ph"][:], in0=P["kf"][:], scalar=-C2, in1=P["ph"][:],
                                                op0=ALU.mult, op1=ALU.add))

            def wrap(nm):
                for _ in range(2):
                    dv(lambda e: e.tensor_scalar(out=P["m"][:], in0=P[nm][:], scalar1=math.pi, scalar2=-TWO_PI,
                                                 op0=ALU.is_gt, op1=ALU.mult))
                    dv(lambda e: e.tensor_tensor(out=P[nm][:], in0=P[nm][:], in1=P["m"][:], op=ALU.add))
                    dv(lambda e: e.tensor_scalar(out=P["m"][:], in0=P[nm][:], scalar1=-math.pi, scalar2=TWO_PI,
                                                 op0=ALU.is_lt, op1=ALU.mult))
                    dv(lambda e: e.tensor_tensor(out=P[nm][:], in0=P[nm][:], in1=P["m"][:], op=ALU.add))
                dv(lambda e: e.tensor_scalar(out=P[nm][:], in0=P[nm][:], scalar1=math.pi, scalar2=-math.pi,
                                             op0=ALU.min, op1=ALU.max))
            wrap("ph")
            dv(lambda e: e.tensor_scalar(out=P["phc"][:], in0=P["ph"][:], scalar1=0.5 * math.pi, scalar2=None,
                                         op0=ALU.add))
            wrap("phc")
            ac(lambda e: e.activation(out=P["sn"][:], in_=P["ph"][:], func=AF.Sin))
            ac(lambda e: e.activation(out=P["cs"][:], in_=P["phc"][:], func=AF.Sin))
            dv(lambda e: e.tensor_tensor(out=P["ar"][:], in0=P["mag"][:], in1=P["cs"][:], op=ALU.mult))
            dv(lambda e: e.tensor_tensor(out=P["ai"][:], in0=P["mag"][:], in1=P["sn"][:], op=ALU.mult))
            dv(lambda e: e.tensor_tensor(out=P["den"][:], in0=lr, in1=lr, op=ALU.mult))
            dv(lambda e: e.tensor_tensor(out=P["t0"][:], in0=lim, in1=lim, op=ALU.mult))
            dv(lambda e: e.tensor_tensor(out=P["den"][:], in0=P["den"][:], in1=P["t0"][:], op=ALU.add))
            dv(lambda e: e.reciprocal(out=P["den"][:], in_=P["den"][:]))
            dv(lambda e: e.tensor_scalar(out=P["am1"][:], in0=P["ar"][:], scalar1=-1.0, scalar2=None, op0=ALU.add))
            dv(lambda e: e.tensor_tensor(out=P["t0"][:], in0=P["am1"][:], in1=lr, op=ALU.mult))
            dv(lambda e: e.tensor_tensor(out=P["t1"][:], in0=P["ai"][:], in1=lim, op=ALU.mult))
            dv(lambda e: e.tensor_tensor(out=P["t0"][:], in0=P["t0"][:], in1=P["t1"][:], op=ALU.add))
            dv(lambda e: e.tensor_tensor(out=P["cr"][:], in0=P["t0"][:], in1=P["den"][:], op=ALU.mult))
            dv(lambda e: e.tensor_tensor(out=P["t0"][:], in0=P["ai"][:], in1=lr, op=ALU.mult))
            dv(lambda e: e.tensor_tensor(out=P["t1"][:], in0=P["am1"][:], in1=lim, op=ALU.mult))
            dv(lambda e: e.tensor_tensor(out=P["t0"][:], in0=P["t0"][:], in1=P["t1"][:], op=ALU.subtract))
            dv(lambda e: e.tensor_tensor(out=P["ci"][:], in0=P["t0"][:], in1=P["den"][:], op=ALU.mult))
            es2 = ExitStack()
            t1 = es2.enter_context(SBT("s5_t1", [128, 16, TC], F32))
            t2 = es2.enter_context(SBT("s5_t2", [128, 16, TC], F32))
            dv(lambda e: e.tensor_copy(out=Er[:, :, 0:1], in_=P["cs"][:].unsqueeze(2)))
            dv(lambda e: e.tensor_copy(out=Ei[:, :, 0:1], in_=P["sn"][:].unsqueeze(2)))
            ln_ = 1
            while ln_ < TC:
                L = ln_
                srb = Er[:, :, L - 1:L].to_broadcast([128, 16, L])
                sib = Ei[:, :, L - 1:L].to_broadcast([128, 16, L])
                dv(lambda e, L=L, srb=srb: e.tensor_tensor(out=t1[:, :, 0:L], in0=Er[:, :, 0:L], in1=srb, op=ALU.mult))
                dv(lambda e, L=L, sib=sib: e.tensor_tensor(out=t2[:, :, 0:L], in0=Ei[:, :, 0:L], in1=sib, op=ALU.mult))
                dv(lambda e, L=L: e.tensor_tensor(out=Er[:, :, L:2 * L], in0=t1[:, :, 0:L], in1=t2[:, :, 0:L],
                                                  op=ALU.subtract))
                dv(lambda e, L=L, sib=sib: e.tensor_tensor(out=t1[:, :, 0:L], in0=Er[:, :, 0:L], in1=sib, op=ALU.mult))
                dv(lambda e, L=L, srb=srb: e.tensor_tensor(out=t2[:, :, 0:L], in0=Ei[:, :, 0:L], in1=srb, op=ALU.mult))
                dv(lambda e, L=L: e.tensor_tensor(out=Ei[:, :, L:2 * L], in0=t1[:, :, 0:L], in1=t2[:, :, 0:L],
                                                  op=ALU.add))
                ln_ *= 2
            crb = P["cr"][:].unsqueeze(2).to_broadcast([128, 16, TC])
            cib = P["ci"][:].unsqueeze(2).to_broadcast([128, 16, TC])
            dv(lambda e: e.tensor_tensor(out=t1[:], in0=Er[:], in1=crb, op=ALU.mult))
            dv(lambda e: e.tensor_tensor(out=t2[:], in0=Ei[:], in1=cib, op=ALU.mult))
            dv(lambda e: e.tensor_tensor(out=Fr[:], in0=t1[:], in1=t2[:], op=ALU.add))
            dv(lambda e: e.tensor_tensor(out=t1[:], in0=Er[:], in1=cib, op=ALU.mult))
            dv(lambda e: e.tensor_tensor(out=t2[:], in0=Ei[:], in1=crb, op=ALU.mult))
            dv(lambda e: e.tensor_tensor(out=Fi[:], in0=t1[:], in1=t2[:], op=ALU.subtract))
            dv(lambda e: e.memset(P["hre"][:], 0.0))
            dv(lambda e: e.memset(P["him"][:], 0.0))
            es2.close()
            kb.barrier()
            it = 0
            for c in range(2):
                wt, wr = ws.get(2048)
                for oo in range(2):
                    o = 2 * c + oo
                    for tt in range(4):
                        b = it % 2
                        it += 1
                        kb.mm([(PS[b][:], [(wt[:, k * 256 + oo * 128:k * 256 + (oo + 1) * 128], xb[:, k, tsl(tt)])
                                           for k in range(8)])],
                              reads=[wr] + [xres[k][tt] for k in range(8)], writes=[PR[b]])
                        kb.op("act", lambda e, b=b, o=o, tt=tt: e.activation(out=uT[:, o, tsl(tt)], in_=PS[b][:],
                                                                             func=AF.Copy),
                              reads=[PR[b]], writes=[uTr[o][tt]])
            esw = ExitStack()

            def sw(name, shape, dt):
                return esw.enter_context(SBT(name, shape, dt))
            WA = [sw("s5w_a%d" % i, [128, 4, TC], F32) for i in range(2)]
            WB = [sw("s5w_b%d" % i, [128, 4, TC], F32) for i in range(2)]
            WC = [sw("s5w_c%d" % i, [128, 4, TC], F32) for i in range(2)]
            WD = [sw("s5w_d%d" % i, [128, 4, TC], F32) for i in range(2)]
            dres = [Res(), Res()]
            WGR = [sw("s5w_gr%d" % i, [128, 4, TC], F32) for i in range(2)]
            WGI = [sw("s5w_gi%d" % i, [128, 4, TC], F32) for i in range(2)]
            WPC = [sw("s5w_pc%d" % i, [128, 4, TC], F32) for i in range(2)]
            WPD = [sw("s5w_pd", [128, 4, TC], F32)] * 2
            MAGT = sw("s5_magt", [128, 16, TC], F32)
            hrb = [sw("s5_hrb%d" % i, [128, 4, TC], BF16) for i in range(2)]
            hib = [sw("s5_hib%d" % i, [128, 4, TC], BF16) for i in range(2)]
            wres = [Res(), Res()]
            ares = [Res(), Res()]
            cres_ = [Res(), Res()]
            risr = [Res(), Res()]
            gres = [Res(), Res()]
            pcr = [Res(), Res()]
            hbres = [Res(), Res()]
            hibres = [Res(), Res()]
            hcr = [Res() for _ in range(4)]
            hci = [Res() for _ in range(4)]
            pdres = Res()
            kb.op("dve", lambda e: e.tensor_copy(out=MAGT[:], in_=P["mag"][:].unsqueeze(2).to_broadcast([128, 16, TC])),
                  reads=[pr], writes=[pr])
            kb.op("dve", lambda e: e.memset(MAGT[:, :, 0:1], 0.0), reads=[pr], writes=[pr])
            ctv = Ct[:].rearrange("p a b c d -> p (a b) c d")[:, :, 1, :]
            kb.op("dve", lambda e: e.tensor_scalar(out=ctv, in0=ctv, scalar1=-1.0, scalar2=None, op0=ALU.mult),
                  reads=[rwr], writes=[rwr])
            L1 = slice(TC - 1, TC)
            F1 = slice(0, 1)

            def unit_steps(ch, o):
                s_ = o % 2
                tt = ch // 4
                csl = slice(ch * TC, (ch + 1) * TC)
                jsl = slice(4 * o, 4 * o + 4)
                R_, I_ = PS[s_], PS[2 + s_]
                A, B_, C_, D_, GR, GI, PC, PD = WA[s_], WB[s_], WC[s_], WD[s_], WGR[s_], WGI[s_], WPC[s_], WPD[s_]
                wr_, ar_, cr_, gr_, pc_, ri_ = wres[s_], ares[s_], cres_[s_], gres[s_], pcr[s_], risr[s_]
                Rv = R_[:].rearrange("p (a b) -> p a b", a=4)
                Iv = I_[:].rearrange("p (a b) -> p a b", a=4)
                flat = "p a b -> p (a b)"

                def pe_in():
                    kb.mm([(R_[:, jj * TC:(jj + 1) * TC], [(Bt[:, 4 * o + jj, 0, :], uT[:, o, csl])]) for jj in range(4)],
                          reads=[rwr, uTr[o][tt]], writes=[PR[s_]])
                    kb.mm([(I_[:, jj * TC:(jj + 1) * TC], [(Bt[:, 4 * o + jj, 1, :], uT[:, o, csl])]) for jj in range(4)],
                          reads=[rwr, uTr[o][tt]], writes=[PR[2 + s_]])

                def head():
                    pass
                dve = []

                def D(fn, reads, writes):
                    dve.append(lambda: kb.op("dve", fn, reads=reads, writes=writes))
                D(lambda e: e.tensor_tensor(out=A[:], in0=Rv, in1=Fr[:, jsl, :], op=ALU.mult), [PR[s_], pr], [ar_])
                D(lambda e: e.tensor_tensor(out=B_[:], in0=Iv, in1=Fi[:, jsl, :], op=ALU.mult), [PR[2 + s_], pr], [wr_])
                D(lambda e: e.tensor_tensor(out=C_[:], in0=Iv, in1=Fr[:, jsl, :], op=ALU.mult), [PR[2 + s_], pr], [cr_])
                D(lambda e: e.tensor_tensor(out=D_[:], in0=Rv, in1=Fi[:, jsl, :], op=ALU.mult), [PR[s_], pr], [dres[s_]])
                D(lambda e: e.tensor_tensor(out=A[:], in0=A[:], in1=B_[:], op=ALU.subtract), [ar_, wr_], [ar_])
                D(lambda e: e.tensor_tensor(out=C_[:], in0=C_[:], in1=D_[:], op=ALU.add), [cr_, dres[s_]], [cr_])
                D(lambda e: e.tensor_tensor(out=A[:, :, F1], in0=A[:, :, F1], in1=P["hre"][:, jsl].unsqueeze(2), op=ALU.add),
                  [ar_, hcr[o]], [ar_])
                D(lambda e: e.tensor_tensor_scan(out=GR[:].rearrange(flat), data0=MAGT[:, jsl, :].rearrange(flat),
                                                 data1=A[:].rearrange(flat), initial=0.0, op0=ALU.mult, op1=ALU.add),
                  [ar_, pr], [gr_])
                D(lambda e: e.tensor_tensor(out=C_[:, :, F1], in0=C_[:, :, F1], in1=P["him"][:, jsl].unsqueeze(2), op=ALU.add),
                  [cr_, hci[o]], [cr_])
                D(lambda e: e.tensor_tensor_scan(out=GI[:].rearrange(flat), data0=MAGT[:, jsl, :].rearrange(flat),
                                                 data1=C_[:].rearrange(flat), initial=0.0, op0=ALU.mult, op1=ALU.add),
                  [cr_, pr], [gr_])
                D(lambda e: e.tensor_tensor(out=A[:], in0=GR[:], in1=Er[:, jsl, :], op=ALU.mult), [gr_, pr], [ar_])
                D(lambda e: e.tensor_tensor(out=B_[:], in0=GI[:], in1=Ei[:, jsl, :], op=ALU.mult), [gr_, pr], [wr_])
                D(lambda e: e.tensor_tensor(out=A[:], in0=A[:], in1=B_[:], op=ALU.subtract), [ar_, wr_], [ar_])
                D(lambda e: e.tensor_tensor(out=P["hre"][:, jsl].unsqueeze(2), in0=A[:, :, L1], in1=P["mag"][:, jsl].unsqueeze(2),
                                            op=ALU.mult), [ar_, pr], [hcr[o]])

                def tail():
                    kb.op("act", lambda e: e.activation(out=hrb[s_][:], in_=A[:], func=AF.Copy), reads=[ar_], writes=[hbres[s_]])
                    kb.op("pool", lambda e: e.tensor_tensor(out=PC[:], in0=GI[:], in1=Er[:, jsl, :], op=ALU.mult),
                          reads=[gr_, pr], writes=[pc_])
                    kb.op("pool", lambda e: e.tensor_tensor(out=PD[:], in0=GR[:], in1=Ei[:, jsl, :], op=ALU.mult),
                          reads=[gr_, pr], writes=[pdres])
                    kb.op("pool", lambda e: e.tensor_tensor(out=PC[:], in0=PC[:], in1=PD[:], op=ALU.add), reads=[pc_, pdres], writes=[pc_])
                    kb.op("pool", lambda e: e.tensor_tensor(out=P["him"][:, jsl].unsqueeze(2), in0=PC[:, :, L1],
                                                            in1=P["mag"][:, jsl].unsqueeze(2), op=ALU.mult),
                          reads=[pc_, pr], writes=[hci[o]])
                    kb.op("act", lambda e: e.activation(out=hib[s_][:], in_=PC[:], func=AF.Copy), reads=[pc_], writes=[hibres[s_]])

                def pe_out():
                    pairs = []
                    for jj in range(4):
                        pairs.append((Ct[:, o, jj, 0, :], hrb[s_][:, jj, :]))
                        pairs.append((Ct[:, o, jj, 1, :], hib[s_][:, jj, :]))
                    pairs.append((Dt[:, o, :], uT[:, o, csl]))
                    q4 = ch % 4
                    kb.mm([(PS[4 + o][:, q4 * TC:(q4 + 1) * TC], pairs)],
                          reads=[rwr, hbres[s_], hibres[s_], uTr[o][tt]], writes=[PR[4 + o]])
                return pe_in, dve, tail, pe_out, head

            plist = [(ch, op_) for ch in range(T // TC) for op_ in range(2)]
            units = {}

            def get_units(pi_):
                if pi_ not in units:
                    ch_, op_ = plist[pi_]
                    units[pi_] = (unit_steps(ch_, 2 * op_), unit_steps(ch_, 2 * op_ + 1))
                return units[pi_]
            u0, u1 = get_units(0)
            u0[0]()
            u1[0]()
            for pi_ in range(len(plist)):
                ch, op_ = plist[pi_]
                tt = ch // 4
                u0, u1 = get_units(pi_)
                u0[4]()
                u1[4]()
                for f0, f1 in zip(u0[1], u1[1]):
                    f0()
                    f1()
                if pi_ + 1 < len(plist):
                    n0, n1 = get_units(pi_ + 1)
                    n0[0]()
                    n1[0]()
                u0[2]()
                u1[2]()
                u0[3]()
                u1[3]()
                del units[pi_]
                if op_ == 1 and ch % 4 == 3:
                    for o in range(4):
                        kb.op("act", lambda e, o=o, tt=tt: e.activation(out=uT[:, o, tsl(tt)], in_=PS[4 + o][:],
                                                                        func=AF.Gelu),
                              reads=[PR[4 + o]] + [uTr[oo][tt] for oo in range(4)], writes=[uTr[o][tt]])
            esw.close()
            kb.barrier()
            sgt = [sb("s5_sg%d" % i, [128, 512], BF16) for i in range(4)]
            sgr = [Res() for _ in range(4)]
            for tt in range(4):
                for o2 in range(4):
                    b = o2 % 2
                    kb.mm([(PS[b][:], [(glu[:, k, o2 * 128:(o2 + 1) * 128], uT[:, k, tsl(tt)]) for k in range(4)])],
                          reads=[rwr] + [uTr[k][tt] for k in range(4)], writes=[PR[b]])
                    kb.op("act", lambda e, b=b, o2=o2: e.activation(out=sgt[o2][:], in_=PS[b][:], func=AF.Sigmoid,
                                                                    bias=sp[:, SP_GLUB + o2:SP_GLUB + o2 + 1]),
                          reads=[PR[b], spres], writes=[sgr[o2]])
                for o2 in range(4):
                    kb.op("dve", lambda e, o2=o2, tt=tt: e.tensor_tensor(out=uT[:, o2, tsl(tt)], in0=uT[:, o2, tsl(tt)],
                                                                         in1=sgt[o2][:], op=ALU.mult),
                          reads=[sgr[o2], uTr[o2][tt]], writes=[uTr[o2][tt]])
            kb.barrier()

    def attn_branch(li, lam_init, yb, ybr):
        with ExitStack() as es:
            def sb(name, shape, dt):
                return es.enter_context(SBT(name, shape, dt))
            wv = sb("at_wv", [128, 8, 512], BF16)
            V = sb("at_V", [128, 16, 512], BF16)
            QT = [sb("at_QT%d" % i, [128, T], BF16) for i in range(2)]
            KT = [[sb("at_KT%d_%d" % (i, mp), [128, T], BF16) for mp in range(2)] for i in range(2)]
            kzr = Res()
            for i in range(2):
                kb.op("pool", lambda e, i=i: e.memset(KT[i][0][64:128, :], 0.0), writes=[kzr])
                kb.op("pool", lambda e, i=i: e.memset(KT[i][1][0:64, :], 0.0), writes=[kzr])
            PT = [sb("at_PT%d" % i, [128, 512], BF16) for i in range(4)]
            o1 = [sb("at_o%d" % i, [128, 512], F32) for i in range(2)]
            rd = sb("at_rd", [128, 512], F32)
            osq = sb("at_osq", [128, 512], BF16)
            oc = sb("at_oc", [128, 512], F32)
            rd2 = sb("at_rd2", [128, 512], F32)
            ocr, rd2r = Res(), Res()
            lq = sb("at_lq", [128, 64], F32)
            lam = sb("at_lam", [128, 4], F32)
            rwr, Vr = Res(), [Res() for _ in range(16)]
            qr = [[Res() for _ in range(4)] for _ in range(2)]
            kr = [[Res() for _ in range(4)] for _ in range(2)]
            ptr = [Res() for _ in range(4)]
            o1r = [Res(), Res()]
            rdr, osqr, lamr = Res(), Res(), Res()
            kb.dma("pool", wv[:].rearrange("p a b -> p (a b)"), resw[li, :, R_AWV:R_AWV + 4096], "rw0", writes=[rwr])
            for n in range(2):
                kb.op("dve", lambda e, n=n: e.tensor_tensor(out=lq[:], in0=sp[:, SP_LQK + 128 * n:SP_LQK + 128 * n + 64],
                                                            in1=sp[:, SP_LQK + 128 * n + 64:SP_LQK + 128 * n + 128],
                                                            op=ALU.mult), reads=[spres, lamr], writes=[lamr])
                kb.op("dve", lambda e, n=n: e.tensor_reduce(out=lam[:, n:n + 1], in_=lq[:], axis=mybir.AxisListType.X,
                                                            op=ALU.add), reads=[lamr], writes=[lamr])
            kb.op("act", lambda e: e.activation(out=lam[:, 0:2], in_=lam[:, 0:2], func=AF.Exp), reads=[lamr], writes=[lamr])
            kb.op("dve", lambda e: e.tensor_tensor(out=lam[:, 2:3], in0=lam[:, 1:2], in1=lam[:, 0:1], op=ALU.subtract),
                  reads=[lamr], writes=[lamr])
            kb.op("dve", lambda e: e.tensor_scalar(out=lam[:, 2:3], in0=lam[:, 2:3], scalar1=-lam_init, scalar2=None,
                                                   op0=ALU.add), reads=[lamr], writes=[lamr])
            kb.op("dve", lambda e: e.tensor_scalar(out=lam[:, 3:4], in0=sp[:, SP_SUB:SP_SUB + 1], scalar1=1.0 - lam_init,
                                                   scalar2=None, op0=ALU.mult), reads=[lamr, spres], writes=[lamr])
            for t16 in range(16):
                b = t16 % 2
                kb.mm([(PS[b][:], [(xb[:, k, t16 * 128:(t16 + 1) * 128], wv[:, k, :]) for k in range(8)])],
                      reads=[rwr] + [xres[k][t16 // 4] for k in range(8)], writes=[PR[b]])
                kb.op("act", lambda e, b=b, t16=t16: e.activation(out=V[:, t16, :], in_=PS[b][:], func=AF.Copy),
                      reads=[PR[b]], writes=[Vr[t16]])
            iters = []
            for h in range(4):
                for qt in range(4):
                    for mp in range(2):
                        for kt in range(4 * (qt + 1)):
                            iters.append((h, qt, mp, kt))
            nit = len(iters)

            def emit_proj(h):
                wt, wr = ws.get(2048)
                Q_, K_ = QT[h % 2], KT[h % 2]
                for tt in range(4):
                    xr = [xres[k][tt] for k in range(8)]
                    kb.mm([(PS[0][:], [(wt[:, k * 256:k * 256 + 128], xb[:, k, tsl(tt)]) for k in range(8)])],
                          reads=[wr] + xr, writes=[PR[0]])
                    kb.op("act", lambda e: e.activation(out=Q_[:, tsl(tt)], in_=PS[0][:], func=AF.Copy, scale=0.125),
                          reads=[PR[0]], writes=[qr[h % 2][tt]])
                    kb.mm([(PS[0][:], [(wt[:, k * 256 + 128:k * 256 + 256], xb[:, k, tsl(tt)]) for k in range(8)])],
                          reads=[wr] + xr, writes=[PR[0]])
                    kb.op("act", lambda e: e.activation(out=K_[0][0:64, tsl(tt)], in_=PS[0][0:64, :], func=AF.Copy),
                          reads=[PR[0], kzr], writes=[kr[h % 2][tt]])
                    kb.op("act", lambda e: e.activation(out=K_[1][64:128, tsl(tt)], in_=PS[0][64:128, :], func=AF.Copy),
                          reads=[PR[0], kzr], writes=[kr[h % 2][tt]])

            def geom(i):
                h, qt, mp, kt = iters[i]
                jd = kt - 4 * qt
                c0 = 128 * jd if jd > 0 else 0
                return h, qt, mp, kt, jd, c0

            def emit_S(i):
                h, qt, mp, kt, jd, c0 = geom(i)
                if qt == 0 and mp == 0 and kt == 0:
                    emit_proj(h)
                sb_ = 1 + (i % 3)
                kb._wait("pe", kb._deps("pe", [kr[h % 2][kt // 4], qr[h % 2][qt], cres], [PR[sb_]]))
                ins = nc.tensor.matmul(PS[sb_][:, c0:512], lhsT=KT[h % 2][mp][:, kt * 128:(kt + 1) * 128],
                                       rhs=QT[h % 2][:, qt * 512 + c0:(qt + 1) * 512], start=True, stop=(jd < 0))
                if jd >= 0:
                    ins = nc.tensor.matmul(PS[sb_][:, c0:c0 + 128], lhsT=ident, rhs=maskneg, start=False, stop=True)
                kb.cnt["pe"] += 1
                ins.then_inc(kb.sem["pe"], 1)
                kb._mark(("pe", kb.cnt["pe"]), [kr[h % 2][kt // 4], qr[h % 2][qt], cres], [PR[sb_]])

            def emit_rest(i):
                h, qt, mp, kt, jd, c0 = geom(i)
                nk = 4 * (qt + 1)
                sb_ = 1 + (i % 3)
                pi = i % 4
                po, pd = PS[4 + 2 * mp], PS[5 + 2 * mp]
                por, pdr = PR[4 + 2 * mp], PR[5 + 2 * mp]
                kb.op("act", lambda e: e.activation(out=PT[pi][:, c0:512], in_=PS[sb_][:, c0:512], func=AF.Exp),
                      reads=[PR[sb_]], writes=[ptr[pi]])
                kb._wait("pe", kb._deps("pe", [ptr[pi], Vr[kt], cres], [por, pdr] if kt == 0 else []))
                nc.tensor.matmul(po[:, c0:512], lhsT=V[:, kt, h * 128:(h + 1) * 128], rhs=PT[pi][:, c0:512],
                                 start=(kt == 0), stop=(kt == nk - 1))
                i2 = nc.tensor.matmul(pd[:, c0:512], lhsT=ones1, rhs=PT[pi][:, c0:512], start=(kt == 0), stop=(kt == nk - 1))
                kb.cnt["pe"] += 1
                i2.then_inc(kb.sem["pe"], 1)
                tok = ("pe", kb.cnt["pe"])
                kb._mark(tok, [ptr[pi], Vr[kt], cres], [])
                por.w = tok
                pdr.w = tok
                if kt == 0:
                    por.r = {}
                    pdr.r = {}
                if kt != nk - 1:
                    return
                kb.op("dve", lambda e: e.reciprocal(out=rd[:], in_=pd[:]), reads=[pdr], writes=[rdr])
                kb.op("dve", lambda e: e.tensor_tensor(out=o1[mp][:], in0=po[:], in1=rd[:], op=ALU.mult),
                      reads=[por, rdr], writes=[o1r[mp]])
                if mp == 0:
                    return
                kb.op("dve", lambda e: e.scalar_tensor_tensor(out=oc[:], in0=o1[1][:], scalar=lam[:, 2:3], in1=o1[0][:],
                                                              op0=ALU.mult, op1=ALU.add),
                      reads=[o1r[0], o1r[1], lamr], writes=[ocr])
                kb.op("dve", lambda e: e.tensor_tensor(out=osq[:], in0=oc[:], in1=oc[:], op=ALU.mult), reads=[ocr], writes=[osqr])

                def st1():
                    kb.mm([(PS[0][:], [(ones128, osq[:])])], reads=[osqr, cres], writes=[PR[0]])
                    kb.op("dve", lambda e: e.tensor_scalar(out=rd2[:], in0=PS[0][:], scalar1=LN_EPS, scalar2=None, op0=ALU.add),
                          reads=[PR[0]], writes=[rd2r])

                def st2():
                    kb.op("act", lambda e: e.activation(out=rd2[:], in_=rd2[:], func=AF.Ln), reads=[rd2r], writes=[rd2r])
                    kb.op("act", lambda e: e.activation(out=rd2[:], in_=rd2[:], func=AF.Exp, scale=-0.5), reads=[rd2r], writes=[rd2r])
                    kb.op("dve", lambda e: e.scalar_tensor_tensor(out=yb[:, h, tsl(qt)], in0=oc[:], scalar=lam[:, 3:4], in1=rd2[:],
                                                                  op0=ALU.mult, op1=ALU.mult),
                          reads=[ocr, rd2r, lamr], writes=[ybr[h][qt]])
                pending.append((i + 3, st1))
                pending.append((i + 5, st2))

            pending = []
            emit_S(0)
            emit_S(1)
            for i in range(nit):
                if i + 2 < nit:
                    emit_S(i + 2)
                while pending and pending[0][0] <= i:
                    pending.pop(0)[1]()
                emit_rest(i)
            while pending:
                pending.pop(0)[1]()
            kb.barrier()

    def gmlp_branch(li, ya, yar):
        with ExitStack() as es:
            def sb(name, shape, dt):
                return es.enter_context(SBT(name, shape, dt))
            wv = sb("gm_wv", [128, 8, 512], BF16)
            wsm = sb("gm_wsm", [128, 4, 128], BF16)
            vln = sb("gm_vln", [128, 16, 512], BF16)
            vg = [sb("gm_vg%d" % i, [128, 512], F32) for i in range(2)]
            junk = sb("gm_junk", [128, 512], F32)
            st = [sb("gm_st%d" % i, [128, 8], F32) for i in range(2)]
            ug = [sb("gm_ug%d" % i, [128, 512], F32) for i in range(2)]
            rwr, wsr = Res(), Res()
            vlr = [Res() for _ in range(16)]
            vgr, str_, ugr = [Res(), Res()], [Res(), Res()], [Res(), Res()]
            junkr = Res()
            kb.dma("pool", wv[:].rearrange("p a b -> p (a b)"), resw[li, :, R_GWV:R_GWV + 4096], "rw0", writes=[rwr])
            kb.op("dve", lambda e: e.tensor_tensor(
                out=wsm[:], in0=sp[:, SP_WST:SP_WST + 512].rearrange("p (g t) -> p g t", g=4),
                in1=tri.unsqueeze(1).to_broadcast([128, 4, 128]), op=ALU.mult), reads=[spres, cres], writes=[wsr])
            gpar = sb("gm_par", [128, 1536], F32)
            gpr = Res()
            kb.dma("sp", gpar[:], smallp[li, :, SP_GLNG:SP_GLNG + 1536], "gpd", writes=[gpr])
            lng = gpar[:, 0:512]
            lnb = gpar[:, 512:1024]
            for t16 in range(16):
                b = t16 % 2
                s_ = st[b]
                kb.mm([(PS[b][:], [(xb[:, k, t16 * 128:(t16 + 1) * 128], wv[:, k, :]) for k in range(8)])],
                      reads=[rwr] + [xres[k][t16 // 4] for k in range(8)], writes=[PR[b]])
                kb.op("act", lambda e, b=b, s_=s_: e.activation(out=vg[b][:], in_=PS[b][:], func=AF.Gelu, accum_out=s_[:, 0:1]),
                      reads=[PR[b]], writes=[vgr[b], str_[b]])
                kb.op("act", lambda e, b=b, s_=s_: e.activation(out=junk[:], in_=vg[b][:], func=AF.Square, accum_out=s_[:, 1:2]),
                      reads=[vgr[b], str_[b]], writes=[junkr, str_[b]])

                def dv(fn, b=b):
                    kb.op("dve", fn, reads=[str_[b]], writes=[str_[b]])
                dv(lambda e, s_=s_: e.tensor_scalar(out=s_[:, 2:4], in0=s_[:, 0:2], scalar1=1.0 / 512.0, scalar2=None, op0=ALU.mult))
                dv(lambda e, s_=s_: e.tensor_tensor(out=s_[:, 4:5], in0=s_[:, 2:3], in1=s_[:, 2:3], op=ALU.mult))
                dv(lambda e, s_=s_: e.tensor_tensor(out=s_[:, 5:6], in0=s_[:, 3:4], in1=s_[:, 4:5], op=ALU.subtract))
                dv(lambda e, s_=s_: e.tensor_scalar(out=s_[:, 5:6], in0=s_[:, 5:6], scalar1=LN_EPS, scalar2=None, op0=ALU.add))
                kb.op("act", lambda e, s_=s_: e.activation(out=s_[:, 6:7], in_=s_[:, 5:6], func=AF.Sqrt),
                      reads=[str_[b]], writes=[str_[b]])
                dv(lambda e, s_=s_: e.reciprocal(out=s_[:, 7:8], in_=s_[:, 6:7]))
                kb.op("dve", lambda e, b=b, s_=s_: e.tensor_scalar(out=vg[b][:], in0=vg[b][:], scalar1=s_[:, 2:3],
                                                                   scalar2=s_[:, 7:8], op0=ALU.subtract, op1=ALU.mult),
                      reads=[vgr[b], str_[b], junkr], writes=[vgr[b]])
                kb.op("dve", lambda e, b=b: e.tensor_tensor(out=vg[b][:], in0=vg[b][:], in1=lng, op=ALU.mult),
                      reads=[vgr[b], gpr], writes=[vgr[b]])
                kb.op("dve", lambda e, b=b, t16=t16: e.tensor_tensor(out=vln[:, t16, :], in0=vg[b][:], in1=lnb, op=ALU.add),
                      reads=[vgr[b], gpr], writes=[vlr[t16]])
            it = 0
            for c in range(2):
                wt, wr = ws.get(2048)
                for oo in range(2):
                    g = 2 * c + oo
                    for tt in range(4):
                        b = it % 2
                        it += 1
                        kb.mm([(PS[4 + b][:], [(wt[:, k * 256 + oo * 128:k * 256 + (oo + 1) * 128], xb[:, k, tsl(tt)])
                                               for k in range(8)])],
                              reads=[wr] + [xres[k][tt] for k in range(8)], writes=[PR[4 + b]])
                        kb.op("act", lambda e, b=b: e.activation(out=ug[b][:], in_=PS[4 + b][:], func=AF.Gelu),
                              reads=[PR[4 + b]], writes=[ugr[b]])
                        kb.mm([(PS[2 + b][:, cc * 128:(cc + 1) * 128], [(vln[:, 4 * tt + cc, g * 128:(g + 1) * 128], wsm[:, g, :])])
                               for cc in range(4)],
                              reads=[wsr] + [vlr[4 * tt + cc] for cc in range(4)], writes=[PR[2 + b]])
                        bsb = gpar[:, 1024 + g * 128:1024 + (g + 1) * 128].unsqueeze(1).to_broadcast([128, 4, 128])
                        kb.op("dve", lambda e, b=b, bsb=bsb: e.tensor_tensor(
                            out=junk[:].rearrange("p (a b) -> p a b", a=4),
                            in0=PS[2 + b][:].rearrange("p (a b) -> p a b", a=4), in1=bsb, op=ALU.add),
                            reads=[PR[2 + b], gpr], writes=[junkr])
                        kb.op("dve", lambda e, b=b, g=g, tt=tt: e.tensor_tensor(out=ya[:, g, tsl(tt)], in0=junk[:], in1=ug[b][:],
                                                                               op=ALU.mult),
                              reads=[junkr, ugr[b]], writes=[yar[g][tt]])
            kb.barrier()

    def final_stage(li, ys, yrs):
        with ExitStack() as es:
            def sb(name, shape, dt):
                return es.enter_context(SBT(name, shape, dt))
            mT = [sb("fs_mT%d" % i, [128, 8, 512], BF16) for i in range(2)]
            rt = [sb("fs_rt%d" % i, [128, 8, 512], F32) for i in range(2)]
            sgt = [sb("fs_sg%d" % i, [128, 512], F32) for i in range(2)]
            bg = BG()
            macc = [sb("fs_macc%d" % i, [128, 512], F32) for i in range(2)]
            mtmp = [sb("fs_mtmp", [128, 512], F32)] * 2
            T_ = ln_temps(es)
            mres = [[Res() for _ in range(8)] for _ in range(2)]
            rres = [[Res() for _ in range(8)] for _ in range(2)]
            sgr = [Res(), Res()]
            maccr, mtmpr = [Res(), Res()], [Res()] * 2
            for tp in range(2):
                for m in range(8):
                    for k in range(3):
                        bg.pump(1)
                        wt, wr = ws.get(1536)
                        for t2 in range(2):
                            tt = 2 * tp + t2
                            pp, gp = PS[t2], PS[2 + t2]
                            kb.mm([(pp[:], [(wt[:, kk * 128:(kk + 1) * 128], ys[k][:, kk, tsl(tt)]) for kk in range(4)])],
                                  reads=[wr] + [yrs[k][kk][tt] for kk in range(4)], writes=[PR[t2]])
                            kb.mm([(gp[:], [(wt[:, 512 + k8 * 128:512 + (k8 + 1) * 128], xb[:, k8, tsl(tt)]) for k8 in range(8)])],
                                  reads=[wr] + [xres[k8][tt] for k8 in range(8)], writes=[PR[2 + t2]])
                            kb.op("act", lambda e: e.activation(out=sgt[t2][:], in_=gp[:], func=AF.Sigmoid),
                                  reads=[PR[2 + t2]], writes=[sgr[t2]])
                            if k == 0:
                                kb.op("dve", lambda e: e.tensor_tensor(out=macc[t2][:], in0=pp[:], in1=sgt[t2][:], op=ALU.mult),
                                      reads=[PR[t2], sgr[t2]], writes=[maccr[t2]])
                            else:
                                kb.op("dve", lambda e: e.tensor_tensor(out=mtmp[t2][:], in0=pp[:], in1=sgt[t2][:], op=ALU.mult),
                                      reads=[PR[t2], sgr[t2]], writes=[mtmpr[t2]])
                                if k == 1:
                                    kb.op("dve", lambda e: e.tensor_tensor(out=macc[t2][:], in0=macc[t2][:], in1=mtmp[t2][:],
                                                                           op=ALU.add),
                                          reads=[maccr[t2], mtmpr[t2]], writes=[maccr[t2]])
                                else:
                                    kb.op("dve", lambda e: e.tensor_tensor(out=mT[t2][:, m, :], in0=macc[t2][:], in1=mtmp[t2][:],
                                                                           op=ALU.add),
                                          reads=[maccr[t2], mtmpr[t2]], writes=[mres[t2][m]])
                for t2 in range(2):
                    tt = 2 * tp + t2
                    for c in range(4):
                        wt, wr = ws.get(2048)
                        for oo in range(2):
                            o = 2 * c + oo
                            b = 4 + (o % 2)
                            kb.mm([(PS[b][:], [(wt[:, k8 * 256 + oo * 128:k8 * 256 + (oo + 1) * 128], mT[t2][:, k8, :])
                                               for k8 in range(8)])],
                                  reads=[wr] + mres[t2], writes=[PR[b]])
                            resid_evac(rt[t2], rres[t2], o, tt, PS[b], PR[b], 1.0 / ALPHA)
                bg.drain()
                for t2 in range(2):
                    bg.add(ln_steps(2 * tp + t2, rt[t2], rres[t2], T_, 1, False))
            bg.drain()
            kb.barrier()

    for li in range(nlayers):
        kb.dma("sp", sp_[:, 0:SP_GLNG], smallp[li, :, 0:SP_GLNG], "spd", writes=[spres])
        kb.dma("sp", sp_[:, SP_GLNG:SPB], smallp[li, :, SP_LQK:SP], "spd", writes=[spres])
        ffn(0, False)
        if stop == "ffn1":
            break
        mixer(li)
        if stop == "mixer":
            break
        ffn(2, li == nlayers - 1)
    if stop is not None:
        with ExitStack() as es:
            xf = es.enter_context(SBT("dbg_xf", [128, 512], F32))
            xfr = Res()
            for c in range(8):
                for tt in range(4):
                    kb.op("dve", lambda e, c=c, tt=tt: e.tensor_tensor(out=xf[:], in0=xb[:, c, tsl(tt)], in1=xlo[:, c, tsl(tt)],
                                                                       op=ALU.add), reads=[xres[c][tt]], writes=[xfr])
                    kb.dma("sp", outT[c * 128:(c + 1) * 128, tsl(tt)], xf[:], "out", reads=[xfr])
    kb.wait_all("sp")
    return nc


_W_KEYS = ["ffn1_gate", "ffn1_up", "ffn1_down", "ln1_g", "ln1_b", "w_in", "gmlp_ln_g", "gmlp_ln_b", "gmlp_ws", "gmlp_bs",
           "diff_lq1", "diff_lk1", "diff_lq2", "diff_lk2", "diff_subln_g", "s5_lambda_re", "s5_lambda_im", "s5_log_step",
           "s5_b_re", "s5_b_im", "s5_c_re", "s5_c_im", "s5_d", "s5_glu_w", "s5_glu_b", "w_branch", "w_out", "ln2_g", "ln2_b",
           "ffn2_gate", "ffn2_up", "ffn2_down", "ln3_g", "ln3_b"]


def pack_all(inputs, nlayers=DEPTH):
    W = {k: np.asarray(inputs[k], dtype=np.float32) for k in _W_KEYS}
    chunks = []
    for i in range(nlayers):
        chunks.extend(pack_layer_stream(W, i))
    wflat = np.ascontiguousarray(np.concatenate(chunks, axis=1))
    resw = np.stack([pack_resident(W, i) for i in range(nlayers)])
    smallp = np.stack([pack_small(W, i) for i in range(nlayers)])
    return wflat, resw, smallp, make_consts()


def kernel(**inputs):
    x = np.asarray(inputs["x"], dtype=np.float32)
    wflat, resw, smallp, consts = pack_all(inputs)
    nc = build_program()
    in_maps = []
    for b in range(x.shape[0]):
        in_maps.append({"xT": np.ascontiguousarray(x[b].T), "wflat": wflat, "resw": resw, "smallp": smallp, "consts": consts})
    res = run_bass_kernel_spmd(nc, in_maps, core_ids=list(range(x.shape[0])))
    out = np.stack([np.ascontiguousarray(r["outT"].T) for r in res.results]).astype(np.float32)
    return out
```

```python
import math
from contextlib import ExitStack
import numpy as np
import concourse.bass as bass
import concourse.mybir as mybir
from concourse.bass_utils import run_bass_kernel_spmd

F32 = mybir.dt.float32
BF16 = mybir.dt.bfloat16
I32 = mybir.dt.int32
AF = mybir.ActivationFunctionType
ALU = mybir.AluOpType

D = 1024
T = 2048
FF = 2816
NJ = 22
DEPTH = 2
LN_EPS = 1e-5
ALPHA = (2 * DEPTH) ** 0.25
SLOT = 2048
NS = 4
HOLD = 2
TC = 128

R_AWV, R_GWV, R_B, R_C, R_D, R_GLU = 0, 4096, 8192, 12288, 16384, 16896
RESW = 18944
SP_LN = 0
SP_GLNG = 48
SP_GLNB = 560
SP_GBS = 1072
SP_LQK = 1584
SP_SUB = 1840
SP_S5 = 1842
SP_GLUB = 1890
SP_WST = 1894
SP = 2408
NCONST = 6 * 128


def _kmaj(w):
    k = w.shape[0] // 128
    n = w.shape[1]
    return w.reshape(k, 128, n).transpose(1, 0, 2).reshape(128, k * n)


def layer_chunk_lens():
    lens = []
    for _ffn in range(1):
        pass

    def ffn():
        l = []
        for half in range(2):
            l += [2048] * NJ
            for rep in range(1 + half):
                for m in range(8):
                    l += [1408, 1408]
        return l

    lens += ffn()
    lens += [2048] * 2
    lens += [2048] * 4
    lens += [2048] * 2
    for tp in range(2):
        lens += [1536] * 24
        lens += [2048] * 8
    lens += ffn()
    return lens


def pack_layer_stream(W, i):
    ch = []

    def ffn(pref):
        wg, wu, wd = W[pref + "_gate"][i], W[pref + "_up"][i], W[pref + "_down"][i]
        ups = [np.concatenate([_kmaj(wg[:, j * 128:(j + 1) * 128]), _kmaj(wu[:, j * 128:(j + 1) * 128])], axis=1)
               for j in range(NJ)]
        downs = []
        for m in range(8):
            for jh in range(2):
                blk = wd[jh * 1408:(jh + 1) * 1408, m * 128:(m + 1) * 128]
                downs.append(_kmaj(blk))
        for half in range(2):
            ch.extend(ups)
            for rep in range(1 + half):
                ch.extend(downs)

    w_in = W["w_in"][i]
    ffn("ffn1")
    for c in range(2):
        ch.append(_kmaj(w_in[:, 2560 + 256 * c:2560 + 256 * (c + 1)]))
    for h in range(4):
        cols = [w_in[:, 1024 + wh * 256 + h * 64:1024 + wh * 256 + (h + 1) * 64] for wh in range(4)]
        ch.append(_kmaj(np.concatenate(cols, axis=1)))
    for c in range(2):
        ch.append(_kmaj(w_in[:, 256 * c:256 * (c + 1)]))
    fin = []
    for m in range(8):
        for k in range(3):
            wb = _kmaj(W["w_branch"][i, k][:, m * 128:(m + 1) * 128])
            wgt = _kmaj(w_in[:, 3072 + k * 1024 + m * 128:3072 + k * 1024 + (m + 1) * 128])
            fin.append(np.concatenate([wb, wgt], axis=1))
    wo = [_kmaj(W["w_out"][i][:, 256 * c:256 * (c + 1)]) for c in range(4)]
    for tp in range(2):
        ch.extend(fin)
        ch.extend(wo)
        ch.extend(wo)
    ffn("ffn2")
    return ch


def pack_resident(W, i):
    r = np.zeros((128, RESW), np.float32)
    w_in = W["w_in"][i]
    r[:, R_AWV:R_AWV + 4096] = _kmaj(w_in[:, 2048:2560])
    r[:, R_GWV:R_GWV + 4096] = _kmaj(w_in[:, 512:1024])
    Bt = np.zeros((128, 16, 2, 128), np.float32)
    Ct = np.zeros((128, 4, 4, 2, 128), np.float32)
    bre, bim = W["s5_b_re"][i], W["s5_b_im"][i]
    cre, cim = W["s5_c_re"][i], W["s5_c_im"][i]
    for j in range(16):
        o, jj = j // 4, j % 4
        for g2 in range(2):
            g = 2 * j + g2
            r0 = (2 * jj + g2) * 16
            Bt[r0:r0 + 16, j, 0, g2 * 64:(g2 + 1) * 64] = bre[g].T
            Bt[r0:r0 + 16, j, 1, g2 * 64:(g2 + 1) * 64] = bim[g].T
            Ct[g2 * 64:(g2 + 1) * 64, o, jj, 0, r0:r0 + 16] = cre[g].T
            Ct[g2 * 64:(g2 + 1) * 64, o, jj, 1, r0:r0 + 16] = cim[g].T
    r[:, R_B:R_B + 4096] = Bt.reshape(128, 4096)
    r[:, R_C:R_C + 4096] = Ct.reshape(128, 4096)
    Dt = np.zeros((128, 4, 128), np.float32)
    dflat = W["s5_d"][i].reshape(512)
    for o in range(4):
        Dt[np.arange(128), o, np.arange(128)] = dflat[o * 128:(o + 1) * 128]
    r[:, R_D:R_D + 512] = Dt.reshape(128, 512)
    r[:, R_GLU:R_GLU + 2048] = _kmaj(W["s5_glu_w"][i])
    return r


def pack_small(W, i):
    s = np.zeros((128, SP), np.float32)
    for n, nm in enumerate(["ln1_g", "ln1_b", "ln2_g", "ln2_b", "ln3_g", "ln3_b"]):
        s[:, SP_LN + 8 * n:SP_LN + 8 * (n + 1)] = W[nm][i].reshape(8, 128).T
    s[:, SP_GLNG:SP_GLNG + 512] = np.broadcast_to(W["gmlp_ln_g"][i][None, :], (128, 512))
    s[:, SP_GLNB:SP_GLNB + 512] = np.broadcast_to(W["gmlp_ln_b"][i][None, :], (128, 512))
    s[:, SP_GBS:SP_GBS + 512] = np.broadcast_to(W["gmlp_bs"][i].reshape(1, 512), (128, 512))
    for n, nm in enumerate(["diff_lq1", "diff_lk1", "diff_lq2", "diff_lk2"]):
        s[:, SP_LQK + 64 * n:SP_LQK + 64 * (n + 1)] = np.broadcast_to(W[nm][i][None, :], (128, 64))
    s[:, SP_SUB] = W["diff_subln_g"][i]
    s[:, SP_S5:SP_S5 + 16] = W["s5_lambda_re"][i].reshape(16, 128).T
    s[:, SP_S5 + 16:SP_S5 + 32] = W["s5_lambda_im"][i].reshape(16, 128).T
    s[:, SP_S5 + 32:SP_S5 + 48] = np.repeat(W["s5_log_step"][i], 64).reshape(16, 128).T
    s[:, SP_GLUB:SP_GLUB + 4] = W["s5_glu_b"][i].reshape(4, 128).T
    s[:, SP_WST:SP_WST + 512] = W["gmlp_ws"][i].transpose(2, 0, 1).reshape(128, 512)
    return s


def make_consts():
    c = np.zeros((128, NCONST), np.float32)
    c[:, 0:128] = 1.0 / 1024.0
    c[:, 128:256] = 1.0 / 128.0
    c[:, 256:384] = 1.0
    c[:, 384:512] = np.eye(128, dtype=np.float32)
    k = np.arange(128)[:, None]
    q = np.arange(128)[None, :]
    c[:, 512:640] = (k <= q).astype(np.float32)
    c[:, 640:768] = np.where(k > q, -30000.0, 0.0).astype(np.float32)
    return c


class Res:
    __slots__ = ("w", "r")

    def __init__(self):
        self.w = None
        self.r = {}


class KB:
    def __init__(self, nc):
        self.nc = nc
        self.E = dict(pe=nc.tensor, act=nc.scalar, dve=nc.vector, pool=nc.gpsimd, sp=nc.sync)
        self.sem = {}
        self.cnt = {}
        self.waited = {e: {} for e in self.E}
        for e in self.E:
            self.newsem(e)

    def newsem(self, key):
        self.sem[key] = self.nc.alloc_semaphore("s_" + key)
        self.cnt[key] = 0

    def _deps(self, eng, reads, writes):
        deps = {}
        for r in reads:
            if r.w is not None:
                k, v = r.w
                if v > deps.get(k, 0):
                    deps[k] = v
        for w in writes:
            if w.w is not None:
                k, v = w.w
                if v > deps.get(k, 0):
                    deps[k] = v
            for k, v in w.r.items():
                if v > deps.get(k, 0):
                    deps[k] = v
        if eng == "pe":
            deps.pop("pe", None)
        return deps

    def _wait(self, eng, deps):
        wd = self.waited[eng]
        for k, v in deps.items():
            if wd.get(k, 0) >= v:
                continue
            self.E[eng].wait_ge(self.sem[k], v)
            wd[k] = v

    def _mark(self, tok, reads, writes):
        k, v = tok
        for w in writes:
            w.w = tok
            w.r = {}
        for r in reads:
            if r.r.get(k, 0) < v:
                r.r[k] = v

    def op(self, eng, fn, reads=(), writes=()):
        self._wait(eng, self._deps(eng, reads, writes))
        inst = fn(self.E[eng])
        self.cnt[eng] += 1
        inst.then_inc(self.sem[eng], 1)
        self._mark((eng, self.cnt[eng]), reads, writes)

    def mm(self, groups, reads=(), writes=()):
        self._wait("pe", self._deps("pe", reads, writes))
        inst = None
        for out_ap, pairs in groups:
            n = len(pairs)
            for i, (l, r) in enumerate(pairs):
                inst = self.nc.tensor.matmul(out_ap, lhsT=l, rhs=r, start=(i == 0), stop=(i == n - 1))
        self.cnt["pe"] += 1
        inst.then_inc(self.sem["pe"], 1)
        self._mark(("pe", self.cnt["pe"]), reads, writes)

    def dma(self, q, out, in_, semkey, reads=(), writes=()):
        if semkey not in self.sem:
            self.newsem(semkey)
        self._wait(q, self._deps(q, reads, writes))
        inst = self.E[q].dma_start(out=out, in_=in_)
        self.cnt[semkey] += 16
        inst.then_inc(self.sem[semkey], 16)
        self._mark((semkey, self.cnt[semkey]), reads, writes)

    def barrier(self):
        for e in self.E:
            deps = {k: v for k, v in self.cnt.items() if k != e and v > 0}
            self._wait(e, deps)

    def wait_all(self, eng):
        deps = {k: v for k, v in self.cnt.items() if k != eng and v > 0}
        self._wait(eng, deps)


class WStream:
    def __init__(self, kb, wflat, lens):
        self.kb = kb
        self.wflat = wflat
        self.lens = lens
        self.offs = np.concatenate([[0], np.cumsum(lens)]).astype(np.int64)
        self.slots = [kb.nc.alloc_sbuf_tensor("wring%d" % s, [128, SLOT], BF16) for s in range(NS)]
        self.res = [Res() for _ in range(NS)]
        self.next_load = 0
        self.next_get = 0

    def get(self, expect_len=None):
        q = self.next_get
        self.next_get += 1
        lim = min(q + NS - HOLD + 1, len(self.lens))
        while self.next_load < lim:
            p = self.next_load
            s = p % NS
            o, l = int(self.offs[p]), int(self.lens[p])
            self.kb.dma("pool", self.slots[s][:, 0:l], self.wflat[:, o:o + l], "w%d" % s, writes=[self.res[s]])
            self.next_load += 1
        if expect_len is not None:
            assert self.lens[q] == expect_len, (q, self.lens[q], expect_len)
        return self.slots[q % NS], self.res[q % NS]


def build_program(nlayers=DEPTH, stop=None):
    nc = bass.Bass("TRN2", target_bir_lowering=False)
    kb = KB(nc)
    lens1 = layer_chunk_lens()
    lens = lens1 * nlayers
    TOT = int(sum(lens))
    xT = nc.dram_tensor("xT", [D, T], F32, kind="ExternalInput").ap()
    wflat = nc.dram_tensor("wflat", [128, TOT], F32, kind="ExternalInput").ap()
    resw = nc.dram_tensor("resw", [nlayers, 128, RESW], F32, kind="ExternalInput").ap()
    smallp = nc.dram_tensor("smallp", [nlayers, 128, SP], F32, kind="ExternalInput").ap()
    constd = nc.dram_tensor("consts", [128, NCONST], F32, kind="ExternalInput").ap()
    outT = nc.dram_tensor("outT", [D, T], F32, kind="ExternalOutput").ap()

    xb = nc.alloc_sbuf_tensor("xb", [128, 8, T], BF16)
    xlo = nc.alloc_sbuf_tensor("xlo", [128, 8, T], BF16)
    xres = [[Res() for _ in range(4)] for _ in range(8)]
    ws = WStream(kb, wflat, lens)
    cb = nc.alloc_sbuf_tensor("cb", [128, NCONST], BF16)
    cres = Res()
    SPB = SP - 1536
    sp_ = nc.alloc_sbuf_tensor("sp", [128, SPB], F32)
    spres = Res()

    class _SPV:
        def __getitem__(self, key):
            p, c = key
            a, b = c.start, c.stop
            assert a < SP_GLNG or a >= SP_LQK, (a, b)
            if a >= SP_LQK:
                a, b = a - 1536, b - 1536
            return sp_[p, a:b]
    sp = _SPV()
    PS = [nc.alloc_psum_tensor("ps%d" % i, [128, 512], F32) for i in range(8)]
    PR = [Res() for _ in range(8)]
    onesD = cb[:, 0:128]
    ones128 = cb[:, 128:256]
    ones1 = cb[:, 256:384]
    tri = cb[:, 512:640]
    ident = cb[:, 384:512]
    maskneg = cb[:, 640:768]

    def tsl(tt):
        return slice(tt * 512, (tt + 1) * 512)

    uid = [0]

    def SBT(name, shape, dt):
        uid[0] += 1
        return nc.sbuf_tensor("%s_u%d" % (name, uid[0]), shape, dt)

    kb.dma("pool", cb[:], constd, "cst", writes=[cres])

    with ExitStack() as es:
        xt = [es.enter_context(SBT("xld%d" % i, [128, 512], F32)) for i in range(4)]
        xtr = [Res() for _ in range(4)]
        n = 0
        for c in range(8):
            for tt in range(4):
                b = n % 4
                n += 1
                kb.dma("sp", xt[b][:], xT[c * 128:(c + 1) * 128, tsl(tt)], "xl%d" % b, writes=[xtr[b]])
                kb.op("act", lambda e, b=b, c=c, tt=tt: e.activation(out=xb[:, c, tsl(tt)], in_=xt[b][:], func=AF.Copy),
                      reads=[xtr[b]], writes=[xres[c][tt]])
                kb.op("dve", lambda e, b=b, c=c, tt=tt: e.tensor_tensor(out=xlo[:, c, tsl(tt)], in0=xt[b][:],
                                                                       in1=xb[:, c, tsl(tt)], op=ALU.subtract),
                      reads=[xtr[b], xres[c][tt]], writes=[xres[c][tt]])
        kb.barrier()

    def resid_evac(rt, rres, m, tt, ps, psr, coef):
        kb.op("dve", lambda e: e.scalar_tensor_tensor(out=rt[:, m, :], in0=ps[:], scalar=coef, in1=xb[:, m, tsl(tt)],
                                                      op0=ALU.mult, op1=ALU.add),
              reads=[psr, xres[m][tt]], writes=[rres[m]])
        kb.op("dve", lambda e: e.tensor_tensor(out=rt[:, m, :], in0=rt[:, m, :], in1=xlo[:, m, tsl(tt)], op=ALU.add),
              reads=[rres[m], xres[m][tt]], writes=[rres[m]])

    class BG:
        def __init__(self):
            self.q = []

        def add(self, steps):
            self.q.extend(steps)

        def pump(self, n=1):
            for _ in range(n):
                if self.q:
                    self.q.pop(0)()

        def drain(self):
            while self.q:
                self.q.pop(0)()

    def ln_steps(tt, rt, rres, T_, gi, final):
        eps = LN_EPS / (ALPHA * ALPHA)
        gcol = sp[:, SP_LN + 16 * gi:SP_LN + 16 * gi + 8]
        bcol = sp[:, SP_LN + 16 * gi + 8:SP_LN + 16 * gi + 16]
        rb, rsq, mu, rstd, tmp, xf = T_["rb"], T_["rsq"], T_["mu"], T_["rstd"], T_["tmp"], T_["xf"]
        sr = T_["sr"]

        def stats(c):
            s = c % 2
            kb.op("dve", lambda e: e.tensor_copy(out=rb[s][:], in_=rt[:, c, :]), reads=[rres[c]], writes=[T_["rbr"][s]])
            kb.op("act", lambda e: e.activation(out=rsq[s][:], in_=rt[:, c, :], func=AF.Square),
                  reads=[rres[c]], writes=[T_["rsqr"][s]])
            kb._wait("pe", kb._deps("pe", [T_["rbr"][s], T_["rsqr"][s], cres], [PR[6], PR[7]] if c == 0 else []))
            nc.tensor.matmul(PS[6][:], lhsT=onesD, rhs=rb[s][:], start=(c == 0), stop=(c == 7))
            i2 = nc.tensor.matmul(PS[7][:], lhsT=onesD, rhs=rsq[s][:], start=(c == 0), stop=(c == 7))
            kb.cnt["pe"] += 1
            i2.then_inc(kb.sem["pe"], 1)
            tok = ("pe", kb.cnt["pe"])
            kb._mark(tok, [T_["rbr"][s], T_["rsqr"][s], cres], [PR[6], PR[7]] if c == 7 else [])
            if c != 7:
                PR[6].w = tok
                PR[7].w = tok

        def stats_lo():
            for c in range(4):
                stats(c)

        def stats_hi():
            for c in range(4, 8):
                stats(c)

        def chain():
            kb.op("dve", lambda e: e.tensor_copy(out=mu[:], in_=PS[6][:]), reads=[PR[6]], writes=[sr])
            kb.op("dve", lambda e: e.tensor_tensor(out=rstd[:], in0=mu[:], in1=mu[:], op=ALU.mult), reads=[sr], writes=[sr])
            kb.op("dve", lambda e: e.tensor_tensor(out=rstd[:], in0=PS[7][:], in1=rstd[:], op=ALU.subtract),
                  reads=[sr, PR[7]], writes=[sr])
            kb.op("dve", lambda e: e.tensor_scalar(out=rstd[:], in0=rstd[:], scalar1=eps, scalar2=None, op0=ALU.add),
                  reads=[sr], writes=[sr])
            kb.op("act", lambda e: e.activation(out=rstd[:], in_=rstd[:], func=AF.Ln), reads=[sr], writes=[sr])
            kb.op("act", lambda e: e.activation(out=rstd[:], in_=rstd[:], func=AF.Exp, scale=-0.5), reads=[sr], writes=[sr])
            kb.op("dve", lambda e: e.scalar_tensor_tensor(out=mu[:], in0=mu[:], scalar=-1.0, in1=rstd[:], op0=ALU.mult,
                                                          op1=ALU.mult), reads=[sr], writes=[sr])

        def norm1(c):
            s = c % 2
            kb.op("dve", lambda e: e.tensor_tensor(out=tmp[s][:], in0=rt[:, c, :], in1=rstd[:], op=ALU.mult),
                  reads=[rres[c], sr], writes=[T_["tmpr"][s]])
            kb.op("dve", lambda e: e.tensor_tensor(out=tmp[s][:], in0=tmp[s][:], in1=mu[:], op=ALU.add),
                  reads=[T_["tmpr"][s], sr], writes=[T_["tmpr"][s]])
            kb.op("act", lambda e: e.activation(out=xf[s][:], in_=tmp[s][:], func=AF.Identity,
                                                scale=gcol[:, c:c + 1], bias=bcol[:, c:c + 1]),
                  reads=[T_["tmpr"][s], spres], writes=[T_["xfr"][s]])
            if final:
                kb.dma("sp", outT[c * 128:(c + 1) * 128, tsl(tt)], xf[s][:], "out", reads=[T_["xfr"][s]])
            else:
                kb.op("act", lambda e: e.activation(out=xb[:, c, tsl(tt)], in_=tmp[s][:], func=AF.Identity,
                                                    scale=gcol[:, c:c + 1], bias=bcol[:, c:c + 1]),
                      reads=[T_["tmpr"][s], spres], writes=[xres[c][tt]])

        def norm2(c):
            s = c % 2
            if not final:
                kb.op("dve", lambda e: e.tensor_tensor(out=xlo[:, c, tsl(tt)], in0=xf[s][:], in1=xb[:, c, tsl(tt)],
                                                       op=ALU.subtract),
                      reads=[T_["xfr"][s], xres[c][tt]], writes=[xres[c][tt]])

        def mk(c):
            def f():
                if c < 8:
                    norm1(c)
                if c >= 1:
                    norm2(c - 1)
            return f
        steps = [stats_lo, stats_hi, chain]
        for c in range(9):
            steps.append(mk(c))
        return steps

    def ln_tile(tt, rt, rres, T_, gi, final):
        for st in ln_steps(tt, rt, rres, T_, gi, final):
            st()

    def ln_temps(es):
        T_ = {}
        T_["rb"] = [es.enter_context(SBT("ln_rb%d" % i, [128, 512], BF16)) for i in range(2)]
        T_["rsq"] = [es.enter_context(SBT("ln_rsq%d" % i, [128, 512], BF16)) for i in range(2)]
        T_["tmp"] = [es.enter_context(SBT("ln_tmp%d" % i, [128, 512], F32)) for i in range(2)]
        T_["xf"] = [es.enter_context(SBT("ln_xf%d" % i, [128, 512], F32)) for i in range(2)]
        T_["mu"] = es.enter_context(SBT("ln_mu", [128, 512], F32))
        T_["rstd"] = es.enter_context(SBT("ln_rstd", [128, 512], F32))
        for k in ("rbr", "rsqr", "tmpr", "xfr"):
            T_[k] = [Res(), Res()]
        T_["sr"] = Res()
        return T_

    def ffn(gi, final):
        with ExitStack() as es:
            h = es.enter_context(SBT("ffn_h", [128, NJ, 1024], BF16))
            sg = [es.enter_context(SBT("ffn_sg%d" % i, [128, 512], BF16)) for i in range(2)]
            rt2 = [es.enter_context(SBT("ffn_rt%d" % i, [128, 8, 512], F32)) for i in range(2)]
            T_ = ln_temps(es)
            hres = [[Res(), Res()] for _ in range(NJ)]
            sgr = [Res(), Res()]
            rres2 = [[Res() for _ in range(8)] for _ in range(2)]
            it = 0
            bg = BG()
            for half in range(2):
                for j in range(NJ):
                    bg.pump(1)
                    wt, wr = ws.get(2048)
                    for tt2 in range(2):
                        tt = 2 * half + tt2
                        b = it % 2
                        it += 1
                        xr = [xres[k][tt] for k in range(8)]
                        kb.mm([(PS[b][:], [(wt[:, k * 128:(k + 1) * 128], xb[:, k, tsl(tt)]) for k in range(8)])],
                              reads=[wr] + xr, writes=[PR[b]])
                        kb.mm([(PS[2 + b][:], [(wt[:, 1024 + k * 128:1024 + (k + 1) * 128], xb[:, k, tsl(tt)])
                                               for k in range(8)])], reads=[wr] + xr, writes=[PR[2 + b]])
                        kb.op("act", lambda e, b=b: e.activation(out=sg[b][:], in_=PS[b][:], func=AF.Silu),
                              reads=[PR[b]], writes=[sgr[b]])
                        kb.op("dve", lambda e, b=b, j=j, tt2=tt2: e.tensor_tensor(
                            out=h[:, j, tt2 * 512:(tt2 + 1) * 512], in0=PS[2 + b][:], in1=sg[b][:], op=ALU.mult),
                            reads=[PR[2 + b], sgr[b]], writes=[hres[j][tt2]])
                passes = [[0, 1]] if half == 0 else [[0], [1]]
                for tts in passes:
                    for m in range(8):
                        bg.pump(2)
                        wa, war = ws.get(1408)
                        wb_, wbr = ws.get(1408)
                        for tt2 in tts:
                            tt = 2 * half + tt2
                            b = 4 + ((m + tt2) % 2)
                            pairs = [(wa[:, jj * 128:(jj + 1) * 128], h[:, jj, tt2 * 512:(tt2 + 1) * 512]) for jj in range(11)]
                            pairs += [(wb_[:, jj * 128:(jj + 1) * 128], h[:, 11 + jj, tt2 * 512:(tt2 + 1) * 512])
                                      for jj in range(11)]
                            kb.mm([(PS[b][:], pairs)], reads=[war, wbr] + [hres[j][tt2] for j in range(NJ)], writes=[PR[b]])
                            resid_evac(rt2[tt2], rres2[tt2], m, tt, PS[b], PR[b], 0.5 / ALPHA)
                    bg.drain()
                    for tt2 in tts:
                        bg.add(ln_steps(2 * half + tt2, rt2[tt2], rres2[tt2], T_, gi, final))
            bg.drain()
            kb.barrier()

    def mixer(li):
        lam_init = 0.8 - 0.6 * math.exp(-0.3 * li)
        with ExitStack() as es0:
            yc = es0.enter_context(SBT("y_c", [128, 4, T], BF16))
            ycr = [[Res() for _ in range(4)] for _ in range(4)]
            s5_branch(li, yc, ycr)
            kb.barrier()
            yb = es0.enter_context(SBT("y_b", [128, 4, T], BF16))
            ybr = [[Res() for _ in range(4)] for _ in range(4)]
            attn_branch(li, lam_init, yb, ybr)
            kb.barrier()
            ya = es0.enter_context(SBT("y_a", [128, 4, T], BF16))
            yar = [[Res() for _ in range(4)] for _ in range(4)]
            gmlp_branch(li, ya, yar)
            kb.barrier()
            final_stage(li, [ya, yb, yc], [yar, ybr, ycr])
            kb.barrier()

    def s5_branch(li, yc, ycr):
        with ExitStack() as es:
            def sb(name, shape, dt):
                return es.enter_context(SBT(name, shape, dt))
            uT = yc
            uTr = ycr
            Er = sb("s5_Er", [128, 16, TC], F32)
            Ei = sb("s5_Ei", [128, 16, TC], F32)
            Fr = sb("s5_Fr", [128, 16, TC], F32)
            Fi = sb("s5_Fi", [128, 16, TC], F32)
            Bt = sb("s5_B", [128, 16, 2, 128], BF16)
            Ct = sb("s5_C", [128, 4, 4, 2, 128], BF16)
            Dt = sb("s5_D", [128, 4, 128], BF16)
            glu = sb("s5_glu", [128, 4, 512], BF16)
            rwr = Res()
            kb.dma("pool", Bt[:].rearrange("p a b c -> p (a b c)"), resw[li, :, R_B:R_B + 4096], "rw0", writes=[rwr])
            kb.dma("pool", Ct[:].rearrange("p a b c d -> p (a b c d)"), resw[li, :, R_C:R_C + 4096], "rw0", writes=[rwr])
            kb.dma("pool", Dt[:].rearrange("p a b -> p (a b)"), resw[li, :, R_D:R_D + 512], "rw0", writes=[rwr])
            kb.dma("pool", glu[:].rearrange("p a b -> p (a b)"), resw[li, :, R_GLU:R_GLU + 2048], "rw0", writes=[rwr])
            P = {}
            for nm in ["step", "mag", "th", "kf", "ph", "phc", "m", "cs", "sn", "ar", "ai", "den", "am1", "cr", "ci",
                       "t0", "t1", "hre", "him"]:
                P[nm] = sb("s5p_" + nm, [128, 16], F32)
            ki = sb("s5p_ki", [128, 16], I32)
            pr = Res()
            lr = sp[:, SP_S5:SP_S5 + 16]
            lim = sp[:, SP_S5 + 16:SP_S5 + 32]
            lsg = sp[:, SP_S5 + 32:SP_S5 + 48]

            def dv(fn):
                kb.op("dve", fn, reads=[pr, spres], writes=[pr])

            def ac(fn):
                kb.op("act", fn, reads=[pr, spres], writes=[pr])
            TWO_PI = 2.0 * math.pi
            C1 = 6.28125
            C2 = TWO_PI - C1
            ac(lambda e: e.activation(out=P["step"][:], in_=lsg, func=AF.Exp))
            dv(lambda e: e.tensor_tensor(out=P["t0"][:], in0=lr, in1=P["step"][:], op=ALU.mult))
            ac(lambda e: e.activation(out=P["mag"][:], in_=P["t0"][:], func=AF.Exp))
            dv(lambda e: e.tensor_tensor(out=P["th"][:], in0=lim, in1=P["step"][:], op=ALU.mult))
            dv(lambda e: e.tensor_scalar(out=P["kf"][:], in0=P["th"][:], scalar1=1.0 / TWO_PI, scalar2=0.5,
                                         op0=ALU.mult, op1=ALU.add))
            dv(lambda e: e.tensor_copy(out=ki[:], in_=P["kf"][:]))
            dv(lambda e: e.tensor_copy(out=P["kf"][:], in_=ki[:]))
            dv(lambda e: e.scalar_tensor_tensor(out=P["ph"][:], in0=P["kf"][:], scalar=-C1, in1=P["th"][:],
                                                op0=ALU.mult, op1=ALU.add))
            dv(lambda e: e.scalar_tensor_tensor(out=P["ph"][:], in0=P["kf"][:], scalar=-C2, in1=P["ph"][:],
                                                op0=ALU.mult, op1=ALU.add))

            def wrap(nm):
                for _ in range(2):
                    dv(lambda e: e.tensor_scalar(out=P["m"][:], in0=P[nm][:], scalar1=math.pi, scalar2=-TWO_PI,
                                                 op0=ALU.is_gt, op1=ALU.mult))
                    dv(lambda e: e.tensor_tensor(out=P[nm][:], in0=P[nm][:], in1=P["m"][:], op=ALU.add))
                    dv(lambda e: e.tensor_scalar(out=P["m"][:], in0=P[nm][:], scalar1=-math.pi, scalar2=TWO_PI,
                                                 op0=ALU.is_lt, op1=ALU.mult))
                    dv(lambda e: e.tensor_tensor(out=P[nm][:], in0=P[nm][:], in1=P["m"][:], op=ALU.add))
                dv(lambda e: e.tensor_scalar(out=P[nm][:], in0=P[nm][:], scalar1=math.pi, scalar2=-math.pi,
                                             op0=ALU.min, op1=ALU.max))
            wrap("ph")
            dv(lambda e: e.tensor_scalar(out=P["phc"][:], in0=P["ph"][:], scalar1=0.5 * math.pi, scalar2=None,
                                         op0=ALU.add))
            wrap("phc")
            ac(lambda e: e.activation(out=P["sn"][:], in_=P["ph"][:], func=AF.Sin))
            ac(lambda e: e.activation(out=P["cs"][:], in_=P["phc"][:], func=AF.Sin))
            dv(lambda e: e.tensor_tensor(out=P["ar"][:], in0=P["mag"][:], in1=P["cs"][:], op=ALU.mult))
            dv(lambda e: e.tensor_tensor(out=P["ai"][:], in0=P["mag"][:], in1=P["sn"][:], op=ALU.mult))
            dv(lambda e: e.tensor_tensor(out=P["den"][:], in0=lr, in1=lr, op=ALU.mult))
            dv(lambda e: e.tensor_tensor(out=P["t0"][:], in0=lim, in1=lim, op=ALU.mult))
            dv(lambda e: e.tensor_tensor(out=P["den"][:], in0=P["den"][:], in1=P["t0"][:], op=ALU.add))
            dv(lambda e: e.reciprocal(out=P["den"][:], in_=P["den"][:]))
            dv(lambda e: e.tensor_scalar(out=P["am1"][:], in0=P["ar"][:], scalar1=-1.0, scalar2=None, op0=ALU.add))
            dv(lambda e: e.tensor_tensor(out=P["t0"][:], in0=P["am1"][:], in1=lr, op=ALU.mult))
            dv(lambda e: e.tensor_tensor(out=P["t1"][:], in0=P["ai"][:], in1=lim, op=ALU.mult))
            dv(lambda e: e.tensor_tensor(out=P["t0"][:], in0=P["t0"][:], in1=P["t1"][:], op=ALU.add))
            dv(lambda e: e.tensor_tensor(out=P["cr"][:], in0=P["t0"][:], in1=P["den"][:], op=ALU.mult))
            dv(lambda e: e.tensor_tensor(out=P["t0"][:], in0=P["ai"][:], in1=lr, op=ALU.mult))
            dv(lambda e: e.tensor_tensor(out=P["t1"][:], in0=P["am1"][:], in1=lim, op=ALU.mult))
            dv(lambda e: e.tensor_tensor(out=P["t0"][:], in0=P["t0"][:], in1=P["t1"][:], op=ALU.subtract))
            dv(lambda e: e.tensor_tensor(out=P["ci"][:], in0=P["t0"][:], in1=P["den"][:], op=ALU.mult))
            es2 = ExitStack()
            t1 = es2.enter_context(SBT("s5_t1", [128, 16, TC], F32))
            t2 = es2.enter_context(SBT("s5_t2", [128, 16, TC], F32))
            dv(lambda e: e.tensor_copy(out=Er[:, :, 0:1], in_=P["cs"][:].unsqueeze(2)))
            dv(lambda e: e.tensor_copy(out=Ei[:, :, 0:1], in_=P["sn"][:].unsqueeze(2)))
            ln_ = 1
            while ln_ < TC:
                L = ln_
                srb = Er[:, :, L - 1:L].to_broadcast([128, 16, L])
                sib = Ei[:, :, L - 1:L].to_broadcast([128, 16, L])
                dv(lambda e, L=L, srb=srb: e.tensor_tensor(out=t1[:, :, 0:L], in0=Er[:, :, 0:L], in1=srb, op=ALU.mult))
                dv(lambda e, L=L, sib=sib: e.tensor_tensor(out=t2[:, :, 0:L], in0=Ei[:, :, 0:L], in1=sib, op=ALU.mult))
                dv(lambda e, L=L: e.tensor_tensor(out=Er[:, :, L:2 * L], in0=t1[:, :, 0:L], in1=t2[:, :, 0:L],
                                                  op=ALU.subtract))
                dv(lambda e, L=L, sib=sib: e.tensor_tensor(out=t1[:, :, 0:L], in0=Er[:, :, 0:L], in1=sib, op=ALU.mult))
                dv(lambda e, L=L, srb=srb: e.tensor_tensor(out=t2[:, :, 0:L], in0=Ei[:, :, 0:L], in1=srb, op=ALU.mult))
                dv(lambda e, L=L: e.tensor_tensor(out=Ei[:, :, L:2 * L], in0=t1[:, :, 0:L], in1=t2[:, :, 0:L],
                                                  op=ALU.add))
                ln_ *= 2
            crb = P["cr"][:].unsqueeze(2).to_broadcast([128, 16, TC])
            cib = P["ci"][:].unsqueeze(2).to_broadcast([128, 16, TC])
            dv(lambda e: e.tensor_tensor(out=t1[:], in0=Er[:], in1=crb, op=ALU.mult))
            dv(lambda e: e.tensor_tensor(out=t2[:], in0=Ei[:], in1=cib, op=ALU.mult))
            dv(lambda e: e.tensor_tensor(out=Fr[:], in0=t1[:], in1=t2[:], op=ALU.add))
            dv(lambda e: e.tensor_tensor(out=t1[:], in0=Er[:], in1=cib, op=ALU.mult))
            dv(lambda e: e.tensor_tensor(out=t2[:], in0=Ei[:], in1=crb, op=ALU.mult))
            dv(lambda e: e.tensor_tensor(out=Fi[:], in0=t1[:], in1=t2[:], op=ALU.subtract))
            dv(lambda e: e.memset(P["hre"][:], 0.0))
            dv(lambda e: e.memset(P["him"][:], 0.0))
            es2.close()
            kb.barrier()
            it = 0
            for c in range(2):
                wt, wr = ws.get(2048)
                for oo in range(2):
                    o = 2 * c + oo
                    for tt in range(4):
                        b = it % 2
                        it += 1
                        kb.mm([(PS[b][:], [(wt[:, k * 256 + oo * 128:k * 256 + (oo + 1) * 128], xb[:, k, tsl(tt)])
                                           for k in range(8)])],
                              reads=[wr] + [xres[k][tt] for k in range(8)], writes=[PR[b]])
                        kb.op("act", lambda e, b=b, o=o, tt=tt: e.activation(out=uT[:, o, tsl(tt)], in_=PS[b][:],
                                                                             func=AF.Copy),
                              reads=[PR[b]], writes=[uTr[o][tt]])
            esw = ExitStack()

            def sw(name, shape, dt):
                return esw.enter_context(SBT(name, shape, dt))
            WA = [sw("s5w_a%d" % i, [128, 4, TC], F32) for i in range(2)]
            WB = [sw("s5w_b%d" % i, [128, 4, TC], F32) for i in range(2)]
            WC = [sw("s5w_c%d" % i, [128, 4, TC], F32) for i in range(2)]
            WD = [sw("s5w_d%d" % i, [128, 4, TC], F32) for i in range(2)]
            dres = [Res(), Res()]
            WGR = [sw("s5w_gr%d" % i, [128, 4, TC], F32) for i in range(2)]
            WGI = [sw("s5w_gi%d" % i, [128, 4, TC], F32) for i in range(2)]
            WPC = [sw("s5w_pc%d" % i, [128, 4, TC], F32) for i in range(2)]
            WPD = [sw("s5w_pd", [128, 4, TC], F32)] * 2
            MAGT = sw("s5_magt", [128, 16, TC], F32)
            hrb = [sw("s5_hrb%d" % i, [128, 4, TC], BF16) for i in range(2)]
            hib = [sw("s5_hib%d" % i, [128, 4, TC], BF16) for i in range(2)]
            wres = [Res(), Res()]
            ares = [Res(), Res()]
            cres_ = [Res(), Res()]
            risr = [Res(), Res()]
            gres = [Res(), Res()]
            pcr = [Res(), Res()]
            hbres = [Res(), Res()]
            hibres = [Res(), Res()]
            hcr = [Res() for _ in range(4)]
            hci = [Res() for _ in range(4)]
            pdres = Res()
            kb.op("dve", lambda e: e.tensor_copy(out=MAGT[:], in_=P["mag"][:].unsqueeze(2).to_broadcast([128, 16, TC])),
                  reads=[pr], writes=[pr])
            kb.op("dve", lambda e: e.memset(MAGT[:, :, 0:1], 0.0), reads=[pr], writes=[pr])
            ctv = Ct[:].rearrange("p a b c d -> p (a b) c d")[:, :, 1, :]
            kb.op("dve", lambda e: e.tensor_scalar(out=ctv, in0=ctv, scalar1=-1.0, scalar2=None, op0=ALU.mult),
                  reads=[rwr], writes=[rwr])
            L1 = slice(TC - 1, TC)
            F1 = slice(0, 1)

            def unit_steps(ch, o):
                s_ = o % 2
                tt = ch // 4
                csl = slice(ch * TC, (ch + 1) * TC)
                jsl = slice(4 * o, 4 * o + 4)
                R_, I_ = PS[s_], PS[2 + s_]
                A, B_, C_, D_, GR, GI, PC, PD = WA[s_], WB[s_], WC[s_], WD[s_], WGR[s_], WGI[s_], WPC[s_], WPD[s_]
                wr_, ar_, cr_, gr_, pc_, ri_ = wres[s_], ares[s_], cres_[s_], gres[s_], pcr[s_], risr[s_]
                Rv = R_[:].rearrange("p (a b) -> p a b", a=4)
                Iv = I_[:].rearrange("p (a b) -> p a b", a=4)
                flat = "p a b -> p (a b)"

                def pe_in():
                    kb.mm([(R_[:, jj * TC:(jj + 1) * TC], [(Bt[:, 4 * o + jj, 0, :], uT[:, o, csl])]) for jj in range(4)],
                          reads=[rwr, uTr[o][tt]], writes=[PR[s_]])
                    kb.mm([(I_[:, jj * TC:(jj + 1) * TC], [(Bt[:, 4 * o + jj, 1, :], uT[:, o, csl])]) for jj in range(4)],
                          reads=[rwr, uTr[o][tt]], writes=[PR[2 + s_]])

                def head():
                    pass
                dve = []

                def D(fn, reads, writes):
                    dve.append(lambda: kb.op("dve", fn, reads=reads, writes=writes))
                D(lambda e: e.tensor_tensor(out=A[:], in0=Rv, in1=Fr[:, jsl, :], op=ALU.mult), [PR[s_], pr], [ar_])
                D(lambda e: e.tensor_tensor(out=B_[:], in0=Iv, in1=Fi[:, jsl, :], op=ALU.mult), [PR[2 + s_], pr], [wr_])
                D(lambda e: e.tensor_tensor(out=C_[:], in0=Iv, in1=Fr[:, jsl, :], op=ALU.mult), [PR[2 + s_], pr], [cr_])
                D(lambda e: e.tensor_tensor(out=D_[:], in0=Rv, in1=Fi[:, jsl, :], op=ALU.mult), [PR[s_], pr], [dres[s_]])
                D(lambda e: e.tensor_tensor(out=A[:], in0=A[:], in1=B_[:], op=ALU.subtract), [ar_, wr_], [ar_])
                D(lambda e: e.tensor_tensor(out=C_[:], in0=C_[:], in1=D_[:], op=ALU.add), [cr_, dres[s_]], [cr_])
                D(lambda e: e.tensor_tensor(out=A[:, :, F1], in0=A[:, :, F1], in1=P["hre"][:, jsl].unsqueeze(2), op=ALU.add),
                  [ar_, hcr[o]], [ar_])
                D(lambda e: e.tensor_tensor_scan(out=GR[:].rearrange(flat), data0=MAGT[:, jsl, :].rearrange(flat),
                                                 data1=A[:].rearrange(flat), initial=0.0, op0=ALU.mult, op1=ALU.add),
                  [ar_, pr], [gr_])
                D(lambda e: e.tensor_tensor(out=C_[:, :, F1], in0=C_[:, :, F1], in1=P["him"][:, jsl].unsqueeze(2), op=ALU.add),
                  [cr_, hci[o]], [cr_])
                D(lambda e: e.tensor_tensor_scan(out=GI[:].rearrange(flat), data0=MAGT[:, jsl, :].rearrange(flat),
                                                 data1=C_[:].rearrange(flat), initial=0.0, op0=ALU.mult, op1=ALU.add),
                  [cr_, pr], [gr_])
                D(lambda e: e.tensor_tensor(out=A[:], in0=GR[:], in1=Er[:, jsl, :], op=ALU.mult), [gr_, pr], [ar_])
                D(lambda e: e.tensor_tensor(out=B_[:], in0=GI[:], in1=Ei[:, jsl, :], op=ALU.mult), [gr_, pr], [wr_])
                D(lambda e: e.tensor_tensor(out=A[:], in0=A[:], in1=B_[:], op=ALU.subtract), [ar_, wr_], [ar_])
                D(lambda e: e.tensor_tensor(out=P["hre"][:, jsl].unsqueeze(2), in0=A[:, :, L1], in1=P["mag"][:, jsl].unsqueeze(2),
                                            op=ALU.mult), [ar_, pr], [hcr[o]])

                def tail():
                    kb.op("act", lambda e: e.activation(out=hrb[s_][:], in_=A[:], func=AF.Copy), reads=[ar_], writes=[hbres[s_]])
                    kb.op("pool", lambda e: e.tensor_tensor(out=PC[:], in0=GI[:], in1=Er[:, jsl, :], op=ALU.mult),
                          reads=[gr_, pr], writes=[pc_])
                    kb.op("pool", lambda e: e.tensor_tensor(out=PD[:], in0=GR[:], in1=Ei[:, jsl, :], op=ALU.mult),
                          reads=[gr_, pr], writes=[pdres])
                    kb.op("pool", lambda e: e.tensor_tensor(out=PC[:], in0=PC[:], in1=PD[:], op=ALU.add), reads=[pc_, pdres], writes=[pc_])
                    kb.op("pool", lambda e: e.tensor_tensor(out=P["him"][:, jsl].unsqueeze(2), in0=PC[:, :, L1],
                                                            in1=P["mag"][:, jsl].unsqueeze(2), op=ALU.mult),
                          reads=[pc_, pr], writes=[hci[o]])
                    kb.op("act", lambda e: e.activation(out=hib[s_][:], in_=PC[:], func=AF.Copy), reads=[pc_], writes=[hibres[s_]])

                def pe_out():
                    pairs = []
                    for jj in range(4):
                        pairs.append((Ct[:, o, jj, 0, :], hrb[s_][:, jj, :]))
                        pairs.append((Ct[:, o, jj, 1, :], hib[s_][:, jj, :]))
                    pairs.append((Dt[:, o, :], uT[:, o, csl]))
                    q4 = ch % 4
                    kb.mm([(PS[4 + o][:, q4 * TC:(q4 + 1) * TC], pairs)],
                          reads=[rwr, hbres[s_], hibres[s_], uTr[o][tt]], writes=[PR[4 + o]])
                return pe_in, dve, tail, pe_out, head

            plist = [(ch, op_) for ch in range(T // TC) for op_ in range(2)]
            units = {}

            def get_units(pi_):
                if pi_ not in units:
                    ch_, op_ = plist[pi_]
                    units[pi_] = (unit_steps(ch_, 2 * op_), unit_steps(ch_, 2 * op_ + 1))
                return units[pi_]
            u0, u1 = get_units(0)
            u0[0]()
            u1[0]()
            for pi_ in range(len(plist)):
                ch, op_ = plist[pi_]
                tt = ch // 4
                u0, u1 = get_units(pi_)
                u0[4]()
                u1[4]()
                for f0, f1 in zip(u0[1], u1[1]):
                    f0()
                    f1()
                if pi_ + 1 < len(plist):
                    n0, n1 = get_units(pi_ + 1)
                    n0[0]()
                    n1[0]()
                u0[2]()
                u1[2]()
                u0[3]()
                u1[3]()
                del units[pi_]
                if op_ == 1 and ch % 4 == 3:
                    for o in range(4):
                        kb.op("act", lambda e, o=o, tt=tt: e.activation(out=uT[:, o, tsl(tt)], in_=PS[4 + o][:],
                                                                        func=AF.Gelu),
                              reads=[PR[4 + o]] + [uTr[oo][tt] for oo in range(4)], writes=[uTr[o][tt]])
            esw.close()
            kb.barrier()
            sgt = [sb("s5_sg%d" % i, [128, 512], BF16) for i in range(4)]
            sgr = [Res() for _ in range(4)]
            for tt in range(4):
                for o2 in range(4):
                    b = o2 % 2
                    kb.mm([(PS[b][:], [(glu[:, k, o2 * 128:(o2 + 1) * 128], uT[:, k, tsl(tt)]) for k in range(4)])],
                          reads=[rwr] + [uTr[k][tt] for k in range(4)], writes=[PR[b]])
                    kb.op("act", lambda e, b=b, o2=o2: e.activation(out=sgt[o2][:], in_=PS[b][:], func=AF.Sigmoid,
                                                                    bias=sp[:, SP_GLUB + o2:SP_GLUB + o2 + 1]),
                          reads=[PR[b], spres], writes=[sgr[o2]])
                for o2 in range(4):
                    kb.op("dve", lambda e, o2=o2, tt=tt: e.tensor_tensor(out=uT[:, o2, tsl(tt)], in0=uT[:, o2, tsl(tt)],
                                                                         in1=sgt[o2][:], op=ALU.mult),
                          reads=[sgr[o2], uTr[o2][tt]], writes=[uTr[o2][tt]])
            kb.barrier()

    def attn_branch(li, lam_init, yb, ybr):
        with ExitStack() as es:
            def sb(name, shape, dt):
                return es.enter_context(SBT(name, shape, dt))
            wv = sb("at_wv", [128, 8, 512], BF16)
            V = sb("at_V", [128, 16, 512], BF16)
            QT = [sb("at_QT%d" % i, [128, T], BF16) for i in range(2)]
            KT = [[sb("at_KT%d_%d" % (i, mp), [128, T], BF16) for mp in range(2)] for i in range(2)]
            kzr = Res()
            for i in range(2):
                kb.op("pool", lambda e, i=i: e.memset(KT[i][0][64:128, :], 0.0), writes=[kzr])
                kb.op("pool", lambda e, i=i: e.memset(KT[i][1][0:64, :], 0.0), writes=[kzr])
            PT = [sb("at_PT%d" % i, [128, 512], BF16) for i in range(4)]
            o1 = [sb("at_o%d" % i, [128, 512], F32) for i in range(2)]
            rd = sb("at_rd", [128, 512], F32)
            osq = sb("at_osq", [128, 512], BF16)
            oc = sb("at_oc", [128, 512], F32)
            rd2 = sb("at_rd2", [128, 512], F32)
            ocr, rd2r = Res(), Res()
            lq = sb("at_lq", [128, 64], F32)
            lam = sb("at_lam", [128, 4], F32)
            rwr, Vr = Res(), [Res() for _ in range(16)]
            qr = [[Res() for _ in range(4)] for _ in range(2)]
            kr = [[Res() for _ in range(4)] for _ in range(2)]
            ptr = [Res() for _ in range(4)]
            o1r = [Res(), Res()]
            rdr, osqr, lamr = Res(), Res(), Res()
            kb.dma("pool", wv[:].rearrange("p a b -> p (a b)"), resw[li, :, R_AWV:R_AWV + 4096], "rw0", writes=[rwr])
            for n in range(2):
                kb.op("dve", lambda e, n=n: e.tensor_tensor(out=lq[:], in0=sp[:, SP_LQK + 128 * n:SP_LQK + 128 * n + 64],
                                                            in1=sp[:, SP_LQK + 128 * n + 64:SP_LQK + 128 * n + 128],
                                                            op=ALU.mult), reads=[spres, lamr], writes=[lamr])
                kb.op("dve", lambda e, n=n: e.tensor_reduce(out=lam[:, n:n + 1], in_=lq[:], axis=mybir.AxisListType.X,
                                                            op=ALU.add), reads=[lamr], writes=[lamr])
            kb.op("act", lambda e: e.activation(out=lam[:, 0:2], in_=lam[:, 0:2], func=AF.Exp), reads=[lamr], writes=[lamr])
            kb.op("dve", lambda e: e.tensor_tensor(out=lam[:, 2:3], in0=lam[:, 1:2], in1=lam[:, 0:1], op=ALU.subtract),
                  reads=[lamr], writes=[lamr])
            kb.op("dve", lambda e: e.tensor_scalar(out=lam[:, 2:3], in0=lam[:, 2:3], scalar1=-lam_init, scalar2=None,
                                                   op0=ALU.add), reads=[lamr], writes=[lamr])
            kb.op("dve", lambda e: e.tensor_scalar(out=lam[:, 3:4], in0=sp[:, SP_SUB:SP_SUB + 1], scalar1=1.0 - lam_init,
                                                   scalar2=None, op0=ALU.mult), reads=[lamr, spres], writes=[lamr])
            for t16 in range(16):
                b = t16 % 2
                kb.mm([(PS[b][:], [(xb[:, k, t16 * 128:(t16 + 1) * 128], wv[:, k, :]) for k in range(8)])],
                      reads=[rwr] + [xres[k][t16 // 4] for k in range(8)], writes=[PR[b]])
                kb.op("act", lambda e, b=b, t16=t16: e.activation(out=V[:, t16, :], in_=PS[b][:], func=AF.Copy),
                      reads=[PR[b]], writes=[Vr[t16]])
            iters = []
            for h in range(4):
                for qt in range(4):
                    for mp in range(2):
                        for kt in range(4 * (qt + 1)):
                            iters.append((h, qt, mp, kt))
            nit = len(iters)

            def emit_proj(h):
                wt, wr = ws.get(2048)
                Q_, K_ = QT[h % 2], KT[h % 2]
                for tt in range(4):
                    xr = [xres[k][tt] for k in range(8)]
                    kb.mm([(PS[0][:], [(wt[:, k * 256:k * 256 + 128], xb[:, k, tsl(tt)]) for k in range(8)])],
                          reads=[wr] + xr, writes=[PR[0]])
                    kb.op("act", lambda e: e.activation(out=Q_[:, tsl(tt)], in_=PS[0][:], func=AF.Copy, scale=0.125),
                          reads=[PR[0]], writes=[qr[h % 2][tt]])
                    kb.mm([(PS[0][:], [(wt[:, k * 256 + 128:k * 256 + 256], xb[:, k, tsl(tt)]) for k in range(8)])],
                          reads=[wr] + xr, writes=[PR[0]])
                    kb.op("act", lambda e: e.activation(out=K_[0][0:64, tsl(tt)], in_=PS[0][0:64, :], func=AF.Copy),
                          reads=[PR[0], kzr], writes=[kr[h % 2][tt]])
                    kb.op("act", lambda e: e.activation(out=K_[1][64:128, tsl(tt)], in_=PS[0][64:128, :], func=AF.Copy),
                          reads=[PR[0], kzr], writes=[kr[h % 2][tt]])

            def geom(i):
                h, qt, mp, kt = iters[i]
                jd = kt - 4 * qt
                c0 = 128 * jd if jd > 0 else 0
                return h, qt, mp, kt, jd, c0

            def emit_S(i):
                h, qt, mp, kt, jd, c0 = geom(i)
                if qt == 0 and mp == 0 and kt == 0:
                    emit_proj(h)
                sb_ = 1 + (i % 3)
                kb._wait("pe", kb._deps("pe", [kr[h % 2][kt // 4], qr[h % 2][qt], cres], [PR[sb_]]))
                ins = nc.tensor.matmul(PS[sb_][:, c0:512], lhsT=KT[h % 2][mp][:, kt * 128:(kt + 1) * 128],
                                       rhs=QT[h % 2][:, qt * 512 + c0:(qt + 1) * 512], start=True, stop=(jd < 0))
                if jd >= 0:
                    ins = nc.tensor.matmul(PS[sb_][:, c0:c0 + 128], lhsT=ident, rhs=maskneg, start=False, stop=True)
                kb.cnt["pe"] += 1
                ins.then_inc(kb.sem["pe"], 1)
                kb._mark(("pe", kb.cnt["pe"]), [kr[h % 2][kt // 4], qr[h % 2][qt], cres], [PR[sb_]])

            def emit_rest(i):
                h, qt, mp, kt, jd, c0 = geom(i)
                nk = 4 * (qt + 1)
                sb_ = 1 + (i % 3)
                pi = i % 4
                po, pd = PS[4 + 2 * mp], PS[5 + 2 * mp]
                por, pdr = PR[4 + 2 * mp], PR[5 + 2 * mp]
                kb.op("act", lambda e: e.activation(out=PT[pi][:, c0:512], in_=PS[sb_][:, c0:512], func=AF.Exp),
                      reads=[PR[sb_]], writes=[ptr[pi]])
                kb._wait("pe", kb._deps("pe", [ptr[pi], Vr[kt], cres], [por, pdr] if kt == 0 else []))
                nc.tensor.matmul(po[:, c0:512], lhsT=V[:, kt, h * 128:(h + 1) * 128], rhs=PT[pi][:, c0:512],
                                 start=(kt == 0), stop=(kt == nk - 1))
                i2 = nc.tensor.matmul(pd[:, c0:512], lhsT=ones1, rhs=PT[pi][:, c0:512], start=(kt == 0), stop=(kt == nk - 1))
                kb.cnt["pe"] += 1
                i2.then_inc(kb.sem["pe"], 1)
                tok = ("pe", kb.cnt["pe"])
                kb._mark(tok, [ptr[pi], Vr[kt], cres], [])
                por.w = tok
                pdr.w = tok
                if kt == 0:
                    por.r = {}
                    pdr.r = {}
                if kt != nk - 1:
                    return
                kb.op("dve", lambda e: e.reciprocal(out=rd[:], in_=pd[:]), reads=[pdr], writes=[rdr])
                kb.op("dve", lambda e: e.tensor_tensor(out=o1[mp][:], in0=po[:], in1=rd[:], op=ALU.mult),
                      reads=[por, rdr], writes=[o1r[mp]])
                if mp == 0:
                    return
                kb.op("dve", lambda e: e.scalar_tensor_tensor(out=oc[:], in0=o1[1][:], scalar=lam[:, 2:3], in1=o1[0][:],
                                                              op0=ALU.mult, op1=ALU.add),
                      reads=[o1r[0], o1r[1], lamr], writes=[ocr])
                kb.op("dve", lambda e: e.tensor_tensor(out=osq[:], in0=oc[:], in1=oc[:], op=ALU.mult), reads=[ocr], writes=[osqr])

                def st1():
                    kb.mm([(PS[0][:], [(ones128, osq[:])])], reads=[osqr, cres], writes=[PR[0]])
                    kb.op("dve", lambda e: e.tensor_scalar(out=rd2[:], in0=PS[0][:], scalar1=LN_EPS, scalar2=None, op0=ALU.add),
                          reads=[PR[0]], writes=[rd2r])

                def st2():
                    kb.op("act", lambda e: e.activation(out=rd2[:], in_=rd2[:], func=AF.Ln), reads=[rd2r], writes=[rd2r])
                    kb.op("act", lambda e: e.activation(out=rd2[:], in_=rd2[:], func=AF.Exp, scale=-0.5), reads=[rd2r], writes=[rd2r])
                    kb.op("dve", lambda e: e.scalar_tensor_tensor(out=yb[:, h, tsl(qt)], in0=oc[:], scalar=lam[:, 3:4], in1=rd2[:],
                                                                  op0=ALU.mult, op1=ALU.mult),
                          reads=[ocr, rd2r, lamr], writes=[ybr[h][qt]])
                pending.append((i + 3, st1))
                pending.append((i + 5, st2))

            pending = []
            emit_S(0)
            emit_S(1)
            for i in range(nit):
                if i + 2 < nit:
                    emit_S(i + 2)
                while pending and pending[0][0] <= i:
                    pending.pop(0)[1]()
                emit_rest(i)
            while pending:
                pending.pop(0)[1]()
            kb.barrier()

    def gmlp_branch(li, ya, yar):
        with ExitStack() as es:
            def sb(name, shape, dt):
                return es.enter_context(SBT(name, shape, dt))
            wv = sb("gm_wv", [128, 8, 512], BF16)
            wsm = sb("gm_wsm", [128, 4, 128], BF16)
            vln = sb("gm_vln", [128, 16, 512], BF16)
            vg = [sb("gm_vg%d" % i, [128, 512], F32) for i in range(2)]
            junk = sb("gm_junk", [128, 512], F32)
            st = [sb("gm_st%d" % i, [128, 8], F32) for i in range(2)]
            ug = [sb("gm_ug%d" % i, [128, 512], F32) for i in range(2)]
            rwr, wsr = Res(), Res()
            vlr = [Res() for _ in range(16)]
            vgr, str_, ugr = [Res(), Res()], [Res(), Res()], [Res(), Res()]
            junkr = Res()
            kb.dma("pool", wv[:].rearrange("p a b -> p (a b)"), resw[li, :, R_GWV:R_GWV + 4096], "rw0", writes=[rwr])
            kb.op("dve", lambda e: e.tensor_tensor(
                out=wsm[:], in0=sp[:, SP_WST:SP_WST + 512].rearrange("p (g t) -> p g t", g=4),
                in1=tri.unsqueeze(1).to_broadcast([128, 4, 128]), op=ALU.mult), reads=[spres, cres], writes=[wsr])
            gpar = sb("gm_par", [128, 1536], F32)
            gpr = Res()
            kb.dma("sp", gpar[:], smallp[li, :, SP_GLNG:SP_GLNG + 1536], "gpd", writes=[gpr])
            lng = gpar[:, 0:512]
            lnb = gpar[:, 512:1024]
            for t16 in range(16):
                b = t16 % 2
                s_ = st[b]
                kb.mm([(PS[b][:], [(xb[:, k, t16 * 128:(t16 + 1) * 128], wv[:, k, :]) for k in range(8)])],
                      reads=[rwr] + [xres[k][t16 // 4] for k in range(8)], writes=[PR[b]])
                kb.op("act", lambda e, b=b, s_=s_: e.activation(out=vg[b][:], in_=PS[b][:], func=AF.Gelu, accum_out=s_[:, 0:1]),
                      reads=[PR[b]], writes=[vgr[b], str_[b]])
                kb.op("act", lambda e, b=b, s_=s_: e.activation(out=junk[:], in_=vg[b][:], func=AF.Square, accum_out=s_[:, 1:2]),
                      reads=[vgr[b], str_[b]], writes=[junkr, str_[b]])

                def dv(fn, b=b):
                    kb.op("dve", fn, reads=[str_[b]], writes=[str_[b]])
                dv(lambda e, s_=s_: e.tensor_scalar(out=s_[:, 2:4], in0=s_[:, 0:2], scalar1=1.0 / 512.0, scalar2=None, op0=ALU.mult))
                dv(lambda e, s_=s_: e.tensor_tensor(out=s_[:, 4:5], in0=s_[:, 2:3], in1=s_[:, 2:3], op=ALU.mult))
                dv(lambda e, s_=s_: e.tensor_tensor(out=s_[:, 5:6], in0=s_[:, 3:4], in1=s_[:, 4:5], op=ALU.subtract))
                dv(lambda e, s_=s_: e.tensor_scalar(out=s_[:, 5:6], in0=s_[:, 5:6], scalar1=LN_EPS, scalar2=None, op0=ALU.add))
                kb.op("act", lambda e, s_=s_: e.activation(out=s_[:, 6:7], in_=s_[:, 5:6], func=AF.Sqrt),
                      reads=[str_[b]], writes=[str_[b]])
                dv(lambda e, s_=s_: e.reciprocal(out=s_[:, 7:8], in_=s_[:, 6:7]))
                kb.op("dve", lambda e, b=b, s_=s_: e.tensor_scalar(out=vg[b][:], in0=vg[b][:], scalar1=s_[:, 2:3],
                                                                   scalar2=s_[:, 7:8], op0=ALU.subtract, op1=ALU.mult),
                      reads=[vgr[b], str_[b], junkr], writes=[vgr[b]])
                kb.op("dve", lambda e, b=b: e.tensor_tensor(out=vg[b][:], in0=vg[b][:], in1=lng, op=ALU.mult),
                      reads=[vgr[b], gpr], writes=[vgr[b]])
                kb.op("dve", lambda e, b=b, t16=t16: e.tensor_tensor(out=vln[:, t16, :], in0=vg[b][:], in1=lnb, op=ALU.add),
                      reads=[vgr[b], gpr], writes=[vlr[t16]])
            it = 0
            for c in range(2):
                wt, wr = ws.get(2048)
                for oo in range(2):
                    g = 2 * c + oo
                    for tt in range(4):
                        b = it % 2
                        it += 1
                        kb.mm([(PS[4 + b][:], [(wt[:, k * 256 + oo * 128:k * 256 + (oo + 1) * 128], xb[:, k, tsl(tt)])
                                               for k in range(8)])],
                              reads=[wr] + [xres[k][tt] for k in range(8)], writes=[PR[4 + b]])
                        kb.op("act", lambda e, b=b: e.activation(out=ug[b][:], in_=PS[4 + b][:], func=AF.Gelu),
                              reads=[PR[4 + b]], writes=[ugr[b]])
                        kb.mm([(PS[2 + b][:, cc * 128:(cc + 1) * 128], [(vln[:, 4 * tt + cc, g * 128:(g + 1) * 128], wsm[:, g, :])])
                               for cc in range(4)],
                              reads=[wsr] + [vlr[4 * tt + cc] for cc in range(4)], writes=[PR[2 + b]])
                        bsb = gpar[:, 1024 + g * 128:1024 + (g + 1) * 128].unsqueeze(1).to_broadcast([128, 4, 128])
                        kb.op("dve", lambda e, b=b, bsb=bsb: e.tensor_tensor(
                            out=junk[:].rearrange("p (a b) -> p a b", a=4),
                            in0=PS[2 + b][:].rearrange("p (a b) -> p a b", a=4), in1=bsb, op=ALU.add),
                            reads=[PR[2 + b], gpr], writes=[junkr])
                        kb.op("dve", lambda e, b=b, g=g, tt=tt: e.tensor_tensor(out=ya[:, g, tsl(tt)], in0=junk[:], in1=ug[b][:],
                                                                               op=ALU.mult),
                              reads=[junkr, ugr[b]], writes=[yar[g][tt]])
            kb.barrier()

    def final_stage(li, ys, yrs):
        with ExitStack() as es:
            def sb(name, shape, dt):
                return es.enter_context(SBT(name, shape, dt))
            mT = [sb("fs_mT%d" % i, [128, 8, 512], BF16) for i in range(2)]
            rt = [sb("fs_rt%d" % i, [128, 8, 512], F32) for i in range(2)]
            sgt = [sb("fs_sg%d" % i, [128, 512], F32) for i in range(2)]
            bg = BG()
            macc = [sb("fs_macc%d" % i, [128, 512], F32) for i in range(2)]
            mtmp = [sb("fs_mtmp", [128, 512], F32)] * 2
            T_ = ln_temps(es)
            mres = [[Res() for _ in range(8)] for _ in range(2)]
            rres = [[Res() for _ in range(8)] for _ in range(2)]
            sgr = [Res(), Res()]
            maccr, mtmpr = [Res(), Res()], [Res()] * 2
            for tp in range(2):
                for m in range(8):
                    for k in range(3):
                        bg.pump(1)
                        wt, wr = ws.get(1536)
                        for t2 in range(2):
                            tt = 2 * tp + t2
                            pp, gp = PS[t2], PS[2 + t2]
                            kb.mm([(pp[:], [(wt[:, kk * 128:(kk + 1) * 128], ys[k][:, kk, tsl(tt)]) for kk in range(4)])],
                                  reads=[wr] + [yrs[k][kk][tt] for kk in range(4)], writes=[PR[t2]])
                            kb.mm([(gp[:], [(wt[:, 512 + k8 * 128:512 + (k8 + 1) * 128], xb[:, k8, tsl(tt)]) for k8 in range(8)])],
                                  reads=[wr] + [xres[k8][tt] for k8 in range(8)], writes=[PR[2 + t2]])
                            kb.op("act", lambda e: e.activation(out=sgt[t2][:], in_=gp[:], func=AF.Sigmoid),
                                  reads=[PR[2 + t2]], writes=[sgr[t2]])
                            if k == 0:
                                kb.op("dve", lambda e: e.tensor_tensor(out=macc[t2][:], in0=pp[:], in1=sgt[t2][:], op=ALU.mult),
                                      reads=[PR[t2], sgr[t2]], writes=[maccr[t2]])
                            else:
                                kb.op("dve", lambda e: e.tensor_tensor(out=mtmp[t2][:], in0=pp[:], in1=sgt[t2][:], op=ALU.mult),
                                      reads=[PR[t2], sgr[t2]], writes=[mtmpr[t2]])
                                if k == 1:
                                    kb.op("dve", lambda e: e.tensor_tensor(out=macc[t2][:], in0=macc[t2][:], in1=mtmp[t2][:],
                                                                           op=ALU.add),
                                          reads=[maccr[t2], mtmpr[t2]], writes=[maccr[t2]])
                                else:
                                    kb.op("dve", lambda e: e.tensor_tensor(out=mT[t2][:, m, :], in0=macc[t2][:], in1=mtmp[t2][:],
                                                                           op=ALU.add),
                                          reads=[maccr[t2], mtmpr[t2]], writes=[mres[t2][m]])
                for t2 in range(2):
                    tt = 2 * tp + t2
                    for c in range(4):
                        wt, wr = ws.get(2048)
                        for oo in range(2):
                            o = 2 * c + oo
                            b = 4 + (o % 2)
                            kb.mm([(PS[b][:], [(wt[:, k8 * 256 + oo * 128:k8 * 256 + (oo + 1) * 128], mT[t2][:, k8, :])
                                               for k8 in range(8)])],
                                  reads=[wr] + mres[t2], writes=[PR[b]])
                            resid_evac(rt[t2], rres[t2], o, tt, PS[b], PR[b], 1.0 / ALPHA)
                bg.drain()
                for t2 in range(2):
                    bg.add(ln_steps(2 * tp + t2, rt[t2], rres[t2], T_, 1, False))
            bg.drain()
            kb.barrier()

    for li in range(nlayers):
        kb.dma("sp", sp_[:, 0:SP_GLNG], smallp[li, :, 0:SP_GLNG], "spd", writes=[spres])
        kb.dma("sp", sp_[:, SP_GLNG:SPB], smallp[li, :, SP_LQK:SP], "spd", writes=[spres])
        ffn(0, False)
        if stop == "ffn1":
            break
        mixer(li)
        if stop == "mixer":
            break
        ffn(2, li == nlayers - 1)
    if stop is not None:
        with ExitStack() as es:
            xf = es.enter_context(SBT("dbg_xf", [128, 512], F32))
            xfr = Res()
            for c in range(8):
                for tt in range(4):
                    kb.op("dve", lambda e, c=c, tt=tt: e.tensor_tensor(out=xf[:], in0=xb[:, c, tsl(tt)], in1=xlo[:, c, tsl(tt)],
                                                                       op=ALU.add), reads=[xres[c][tt]], writes=[xfr])
                    kb.dma("sp", outT[c * 128:(c + 1) * 128, tsl(tt)], xf[:], "out", reads=[xfr])
    kb.wait_all("sp")
    return nc


_W_KEYS = ["ffn1_gate", "ffn1_up", "ffn1_down", "ln1_g", "ln1_b", "w_in", "gmlp_ln_g", "gmlp_ln_b", "gmlp_ws", "gmlp_bs",
           "diff_lq1", "diff_lk1", "diff_lq2", "diff_lk2", "diff_subln_g", "s5_lambda_re", "s5_lambda_im", "s5_log_step",
           "s5_b_re", "s5_b_im", "s5_c_re", "s5_c_im", "s5_d", "s5_glu_w", "s5_glu_b", "w_branch", "w_out", "ln2_g", "ln2_b",
           "ffn2_gate", "ffn2_up", "ffn2_down", "ln3_g", "ln3_b"]


def pack_all(inputs, nlayers=DEPTH):
    W = {k: np.asarray(inputs[k], dtype=np.float32) for k in _W_KEYS}
    chunks = []
    for i in range(nlayers):
        chunks.extend(pack_layer_stream(W, i))
    wflat = np.ascontiguousarray(np.concatenate(chunks, axis=1))
    resw = np.stack([pack_resident(W, i) for i in range(nlayers)])
    smallp = np.stack([pack_small(W, i) for i in range(nlayers)])
    return wflat, resw, smallp, make_consts()


def kernel(**inputs):
    x = np.asarray(inputs["x"], dtype=np.float32)
    wflat, resw, smallp, consts = pack_all(inputs)
    nc = build_program()
    in_maps = []
    for b in range(x.shape[0]):
        in_maps.append({"xT": np.ascontiguousarray(x[b].T), "wflat": wflat, "resw": resw, "smallp": smallp, "consts": consts})
    res = run_bass_kernel_spmd(nc, in_maps, core_ids=list(range(x.shape[0])))
    out = np.stack([np.ascontiguousarray(r["outT"].T) for r in res.results]).astype(np.float32)
    return out
```
